# Optimizing a Trainium2 kernel written in Bass

```python
import jax, jax.numpy as jnp
from jax import lax
import numpy as np

D_MODEL = 1024
BATCH = 8
SEQ = 2048
DEPTH = 4

GRID_W = 64
CTX_LEN = 256
N_MIXERS = 2
N_HEADS = 16
N_KV_HEADS = 4
HEAD_DIM = D_MODEL // N_HEADS
GROUP = N_HEADS // N_KV_HEADS
Q_DIM = N_HEADS * HEAD_DIM
KV_DIM = N_KV_HEADS * HEAD_DIM
WINDOW = 128
BLOCK = 128
ROPE_THETA = 10000.0
ROPE_QUARTER = HEAD_DIM // 4
CONV_WIDTH = 3
D_FF = 2816
N_MOD = 9
RMS_EPS = 1e-6
NEG_INF = -1e30
HALF_STEP = 0.5

kernel_name = 'hybrid_shortconv_swa_macaron_dit'


def rmsnorm(x, g):
    xf = x.astype(jnp.float32)
    y = xf * lax.rsqrt(jnp.mean(xf * xf, axis=-1, keepdims=True) + RMS_EPS)
    return (y * g.astype(jnp.float32)).astype(x.dtype)


def adaln_in(x, g, shift, scale):
    return rmsnorm(x, g) * (1 + scale) + shift


def adaln_out(x, y, g, gate, weight):
    return x + weight * gate * rmsnorm(y, g)


def swiglu(h, w_gu, w_down):
    gu = h @ w_gu
    return (jax.nn.silu(gu[..., :D_FF]) * gu[..., D_FF:]) @ w_down


def ffn_sublayer(x, mod, k, g_pre, g_post, w_gu, w_down):
    h = adaln_in(x, g_pre, mod[k], mod[k + 1])
    return adaln_out(x, swiglu(h, w_gu, w_down), g_post, mod[k + 2], HALF_STEP)


def short_conv_mixer(h, w_in, w_conv, w_out):
    bcu = h @ w_in
    b, cg, u = bcu[..., :D_MODEL], bcu[..., D_MODEL:2 * D_MODEL], bcu[..., 2 * D_MODEL:]
    up = jnp.pad(cg * u, ((0, 0), (1, 1), (0, 0)))
    y = w_conv[0] * up[:, :-2] + w_conv[1] * up[:, 1:-1] + w_conv[2] * up[:, 2:]
    return (b * y) @ w_out


def axial_rope_tables(n_tokens, dtype):
    rows = n_tokens // GRID_W
    row = jnp.broadcast_to(jnp.arange(rows)[:, None], (rows, GRID_W)).reshape(-1).astype(jnp.float32)
    col = jnp.broadcast_to(jnp.arange(GRID_W)[None, :], (rows, GRID_W)).reshape(-1).astype(jnp.float32)
    inv_freq = ROPE_THETA ** (-jnp.arange(ROPE_QUARTER, dtype=jnp.float32) / ROPE_QUARTER)
    ang = jnp.stack([row[:, None] * inv_freq, col[:, None] * inv_freq], axis=1)
    return jnp.cos(ang)[:, None].astype(dtype), jnp.sin(ang)[:, None].astype(dtype)


def apply_rope(t, cos, sin):
    t4 = t.reshape(t.shape[:-1] + (2, 2, ROPE_QUARTER))
    x1, x2 = t4[..., 0, :], t4[..., 1, :]
    out = jnp.stack([x1 * cos - x2 * sin, x2 * cos + x1 * sin], axis=-2)
    return out.reshape(t.shape)


def banded_blocks(t, nb):
    B = t.shape[0]
    tp = jnp.pad(t, ((0, 0), (BLOCK, BLOCK), (0, 0), (0, 0)))
    tp = tp.reshape(B, nb + 2, BLOCK, N_KV_HEADS, HEAD_DIM)
    return jnp.concatenate([tp[:, :-2], tp[:, 1:-1], tp[:, 2:]], axis=2)


def band_mask(nb, n_tokens):
    a = jnp.arange(BLOCK)[:, None]
    b = jnp.arange(3 * BLOCK)[None, :]
    off = b - a
    in_win = (off >= BLOCK - WINDOW) & (off <= BLOCK + WINDOW)
    key_pos = (jnp.arange(nb)[:, None] - 1) * BLOCK + b
    valid = (key_pos >= 0) & (key_pos < n_tokens)
    return in_win[None] & valid[:, None, :]


def window_attention_mixer(hx, hz, w_qkv, w_o, sink, cos, sin, ctx_queries):
    B, S, _ = hx.shape
    C = hz.shape[1]
    nb = S // BLOCK
    nk = 3 * BLOCK
    scale = HEAD_DIM ** -0.5
    qkv = hx @ w_qkv
    q = apply_rope(qkv[..., :Q_DIM].reshape(B, S, N_HEADS, HEAD_DIM), cos, sin) * scale
    k = apply_rope(qkv[..., Q_DIM:Q_DIM + KV_DIM].reshape(B, S, N_KV_HEADS, HEAD_DIM), cos, sin)
    v = qkv[..., Q_DIM + KV_DIM:].reshape(B, S, N_KV_HEADS, HEAD_DIM)
    kvz = hz @ w_qkv[:, Q_DIM:]
    kz = kvz[..., :KV_DIM].reshape(B, C, N_KV_HEADS, HEAD_DIM)
    vz = kvz[..., KV_DIM:].reshape(B, C, N_KV_HEADS, HEAD_DIM)
    sink_l = sink.reshape(N_KV_HEADS, GROUP).astype(jnp.float32)

    qb = q.reshape(B, nb, BLOCK, N_KV_HEADS, GROUP, HEAD_DIM)
    kb = banded_blocks(k, nb)
    vb = banded_blocks(v, nb)
    s_win = jnp.einsum('bnqhgd,bnkhd->bnhgqk', qb, kb).astype(jnp.float32)
    s_win = jnp.where(band_mask(nb, S)[None, :, None, None], s_win, NEG_INF)
    s_ctx = jnp.einsum('bnqhgd,bchd->bnhgqc', qb, kz).astype(jnp.float32)
    s_sink = jnp.broadcast_to(sink_l[None, None, :, :, None, None], s_win.shape[:-1] + (1,))
    p = jax.nn.softmax(jnp.concatenate([s_win, s_ctx, s_sink], axis=-1), axis=-1).astype(v.dtype)
    o = (jnp.einsum('bnhgqk,bnkhd->bnqhgd', p[..., :nk], vb)
         + jnp.einsum('bnhgqc,bchd->bnqhgd', p[..., nk:nk + C], vz))
    yx = o.reshape(B, S, Q_DIM) @ w_o

    yz = None
    if ctx_queries:
        qz = (hz @ w_qkv[:, :Q_DIM]).reshape(B, C, N_KV_HEADS, GROUP, HEAD_DIM) * scale
        sz = jnp.einsum('bqhgd,bkhd->bhgqk', qz, kz).astype(jnp.float32)
        sz_sink = jnp.broadcast_to(sink_l[None, :, :, None, None], sz.shape[:-1] + (1,))
        pz = jax.nn.softmax(jnp.concatenate([sz, sz_sink], axis=-1), axis=-1).astype(vz.dtype)
        oz = jnp.einsum('bhgqk,bkhd->bqhgd', pz[..., :C], vz)
        yz = oz.reshape(B, C, Q_DIM) @ w_o
    return yx, yz


def setup_inputs(seed: int = 0) -> dict:
    key = jax.random.key(seed)
    ks = jax.random.split(key, 16)
    n_conv = (DEPTH + 1) // N_MIXERS
    n_attn = DEPTH // N_MIXERS

    def w(k, shape, fan_in, gain=1.0):
        return gain * fan_in ** -0.5 * jax.random.normal(k, shape, jnp.float32)

    return {
        'x': jax.random.normal(ks[0], (BATCH, SEQ, D_MODEL), jnp.float32),
        'c': jax.random.normal(ks[1], (BATCH, D_MODEL), jnp.float32),
        'ctx': jax.random.normal(ks[2], (BATCH, CTX_LEN, D_MODEL), jnp.float32),
        'c_ctx': jax.random.normal(ks[3], (D_MODEL,), jnp.float32),
        'w_mod': w(ks[4], (DEPTH, D_MODEL, N_MOD * D_MODEL), D_MODEL, 0.5),
        'b_mod': 0.02 * jax.random.normal(ks[5], (DEPTH, N_MOD * D_MODEL), jnp.float32),
        'norm_g': 1.0 + 0.05 * jax.random.normal(ks[6], (DEPTH, 6, D_MODEL), jnp.float32),
        'ffn_w_gu': w(ks[7], (DEPTH, 2, D_MODEL, 2 * D_FF), D_MODEL),
        'ffn_w_down': w(ks[8], (DEPTH, 2, D_FF, D_MODEL), D_FF),
        'conv_w_in': w(ks[9], (n_conv, D_MODEL, 3 * D_MODEL), D_MODEL),
        'conv_w': w(ks[10], (n_conv, CONV_WIDTH, D_MODEL), CONV_WIDTH),
        'conv_w_out': w(ks[11], (n_conv, D_MODEL, D_MODEL), D_MODEL),
        'attn_w_qkv': w(ks[12], (n_attn, D_MODEL, Q_DIM + 2 * KV_DIM), D_MODEL),
        'attn_w_o': w(ks[13], (n_attn, Q_DIM, D_MODEL), Q_DIM),
        'attn_sink': 0.5 * jax.random.normal(ks[14], (n_attn, N_HEADS), jnp.float32),
    }


def reference(x, c, ctx, c_ctx, w_mod, b_mod, norm_g, ffn_w_gu, ffn_w_down,
              conv_w_in, conv_w, conv_w_out, attn_w_qkv, attn_w_o, attn_sink):
    B, S, D = x.shape
    cos, sin = axial_rope_tables(S, x.dtype)
    z = ctx
    for i in range(DEPTH):
        last = i == DEPTH - 1
        use_attn = (i % N_MIXERS) == 1
        j = i // N_MIXERS
        g = norm_g[i]
        mx = (jax.nn.silu(c) @ w_mod[i] + b_mod[i]).reshape(B, N_MOD, 1, D).transpose(1, 0, 2, 3)
        mz = (jax.nn.silu(c_ctx) @ w_mod[i] + b_mod[i]).reshape(N_MOD, D)
        ctx_needed = (not last) or use_attn

        x = ffn_sublayer(x, mx, 0, g[0], g[1], ffn_w_gu[i, 0], ffn_w_down[i, 0])
        if ctx_needed:
            z = ffn_sublayer(z, mz, 0, g[0], g[1], ffn_w_gu[i, 0], ffn_w_down[i, 0])

        hx = adaln_in(x, g[2], mx[3], mx[4])
        if use_attn:
            hz = adaln_in(z, g[2], mz[3], mz[4])
            yx, yz = window_attention_mixer(hx, hz, attn_w_qkv[j], attn_w_o[j], attn_sink[j],
                                            cos, sin, not last)
        else:
            yx = short_conv_mixer(hx, conv_w_in[j], conv_w[j], conv_w_out[j])
            yz = None
            if not last:
                hz = adaln_in(z, g[2], mz[3], mz[4])
                yz = short_conv_mixer(hz, conv_w_in[j], conv_w[j], conv_w_out[j])
        x = adaln_out(x, yx, g[3], mx[5], 1.0)

        x = ffn_sublayer(x, mx, 6, g[4], g[5], ffn_w_gu[i, 1], ffn_w_down[i, 1])
        if not last:
            z = adaln_out(z, yz, g[3], mz[5], 1.0)
            z = ffn_sublayer(z, mz, 6, g[4], g[5], ffn_w_gu[i, 1], ffn_w_down[i, 1])
    return x
```

```python
import numpy as np
import concourse.bass as bass
import concourse.mybir as mybir
from concourse.bass_utils import run_bass_kernel_spmd

F32, BF16 = mybir.dt.float32, mybir.dt.bfloat16
AF = mybir.ActivationFunctionType
ALU = mybir.AluOpType

D = 1024
NLAT = 2048
NZ = 256
NTOK = NLAT + NZ
DFF = 2816
NF = DFF // 128
DEPTH = 4
EPS = 1e-6
TILES = [(0, 512), (512, 512), (1024, 512), (1536, 512), (2048, 256)]
GROUPS = [[0, 1, 4], [2, 3]]
HE = [0, 1, 2, 3, 8, 9, 10, 11]
HO = [4, 5, 6, 7, 12, 13, 14, 15]
import os
DBG = os.environ.get('KDBG', '')
ARENA_B = 97280
WAR_B = 27648


class Op:
    __slots__ = ("eng", "fn", "reads", "writes", "dsem", "waits", "signal", "val", "deps", "fence")

    def __init__(self, eng, fn, reads, writes, dsem=None):
        self.eng = eng
        self.fn = fn
        self.reads = reads
        self.writes = writes
        self.dsem = dsem
        self.waits = []
        self.signal = False
        self.val = 0
        self.deps = ()
        self.fence = False


class Prog:
    ENGS = ("pe", "act", "dve", "pool", "sp")

    def __init__(self):
        self.ops = []
        self.pending_fence = False

    def op(self, eng, fn, r=(), w=()):
        o = Op(eng, fn, tuple(r), tuple(w))
        if self.pending_fence:
            pass
        self.ops.append(o)
        return o

    def dma(self, eng, fn, dsem, r=(), w=()):
        o = Op(eng, fn, tuple(r), tuple(w), dsem=dsem)
        self.ops.append(o)
        return o

    def fence(self):
        o = Op(None, None, (), ())
        o.fence = True
        self.ops.append(o)

    def finalize(self):
        ops = self.ops
        last_w = {}
        readers = {}
        last_on_eng = {}
        outstanding_dma = []
        fence_deps = None
        fenced = {e: True for e in self.ENGS}
        all_dma = []
        for i, o in enumerate(ops):
            if o.fence:
                fence_deps = set(last_on_eng.values()) | set(all_dma)
                all_dma = []
                fenced = {e: False for e in self.ENGS}
                continue
            deps = set()
            for k in o.reads:
                j = last_w.get(k)
                if j is not None:
                    deps.add(j)
            for k in o.writes:
                j = last_w.get(k)
                if j is not None:
                    deps.add(j)
                for j in readers.get(k, ()):
                    deps.add(j)
            if fence_deps is not None and not fenced[o.eng]:
                deps |= fence_deps
                fenced[o.eng] = True
            deps.discard(i)
            o.deps = deps
            for k in o.reads:
                readers.setdefault(k, []).append(i)
            for k in o.writes:
                last_w[k] = i
                readers[k] = []
            if o.dsem is None:
                last_on_eng[o.eng] = i
            else:
                all_dma.append(i)
        for i, o in enumerate(ops):
            if o.fence:
                continue
            for j in o.deps:
                p = ops[j]
                if p.dsem is not None:
                    p.signal = True
                elif p.eng == o.eng and p.eng in ("pe", "sp"):
                    continue
                else:
                    p.signal = True
        cnt = {}
        for o in ops:
            if o.fence:
                continue
            if o.dsem is not None:
                cnt[o.dsem] = cnt.get(o.dsem, 0) + 16
                o.val = cnt[o.dsem]
                o.signal = True
            elif o.signal:
                cnt[o.eng] = cnt.get(o.eng, 0) + 1
                o.val = cnt[o.eng]
        seen = {e: {} for e in self.ENGS}
        for o in ops:
            if o.fence:
                continue
            need = {}
            for j in o.deps:
                p = ops[j]
                if p.dsem is not None:
                    key = ("d", p.dsem)
                elif p.eng == o.eng and p.eng in ("pe", "sp"):
                    continue
                else:
                    key = ("e", p.eng)
                if p.val > need.get(key, 0):
                    need[key] = p.val
            sn = seen[o.eng]
            for key, v in need.items():
                if sn.get(key, 0) >= v:
                    continue
                sn[key] = v
                o.waits.append((key, v))
        self.dsems = sorted({o.dsem for o in ops if (not o.fence) and o.dsem is not None})
        self.maxcnt = cnt

    def emit(self, nc, block, sems):
        by_eng = {e: [] for e in self.ENGS}
        for o in self.ops:
            if not o.fence:
                by_eng[o.eng].append(o)

        def run(e, name):
            for o in by_eng[name]:
                for key, v in o.waits:
                    e.wait_ge(sems[key], v)
                if o.fn is None:
                    continue
                ins = o.fn(e)
                if o.signal:
                    if o.dsem is not None:
                        ins.then_inc(sems[("d", o.dsem)], 16)
                    else:
                        ins.then_inc(sems[("e", name)], 1)

        @block.tensor
        def _(e):
            run(e, "pe")

        @block.scalar
        def _(e):
            run(e, "act")

        @block.vector
        def _(e):
            run(e, "dve")

        @block.gpsimd
        def _(e):
            run(e, "pool")

        @block.sync
        def _(e):
            run(e, "sp")


def build_program(n_layers=DEPTH, stop_after=None, force_attn=False):
    nc = bass.Bass("TRN2", target_bir_lowering=False)
    dt = nc.dram_tensor
    xz_d = dt("xz", [NTOK, D], F32, kind="ExternalInput").ap()
    cc_d = dt("ccT", [128, 8, 2], F32, kind="ExternalInput").ap()
    NL = n_layers
    N2 = max(1, (NL + 1) // 2)
    wmod_d = dt("w_mod", [NL, D, 9 * D], F32, kind="ExternalInput").ap()
    bmod_d = dt("bmodT", [NL, 128, 72], F32, kind="ExternalInput").ap()
    g_d = dt("gT", [NL, 128, 6, 8], F32, kind="ExternalInput").ap()
    wgu_d = dt("ffn_w_gu", [NL, 2, D, 2 * DFF], F32, kind="ExternalInput").ap()
    wdn_d = dt("ffn_w_down", [NL, 2, DFF, D], F32, kind="ExternalInput").ap()
    cwin_d = dt("conv_w_in", [N2, D, 3 * D], F32, kind="ExternalInput").ap()
    cw_d = dt("convwT", [N2, 128, 3, 8], F32, kind="ExternalInput").ap()
    cwout_d = dt("conv_w_out", [N2, D, D], F32, kind="ExternalInput").ap()
    wqk_d = dt("wqk", [N2, D, 10, 2, 128], F32, kind="ExternalInput").ap()
    wv_d = dt("wv", [N2, D, 256], F32, kind="ExternalInput").ap()
    wo_d = dt("wo", [N2, D, D], F32, kind="ExternalInput").ap()
    sink_d = dt("sinkB", [N2, 128, 16], F32, kind="ExternalInput").ap()
    cos_d = dt("cosT", [128, NTOK], F32, kind="ExternalInput").ap()
    sin_d = dt("sinT", [128, NTOK], F32, kind="ExternalInput").ap()
    mask_d = dt("masks", [128, 2, 128], F32, kind="ExternalInput").ap()
    id_d = dt("ident", [128, 128], F32, kind="ExternalInput").ap()
    out_d = dt("out", [NLAT, D], F32, kind="ExternalOutput").ap()

    P = Prog()
    from contextlib import ExitStack
    es = ExitStack()
    sb = lambda name, shape, d: es.enter_context(nc.sbuf_tensor(name, shape, d))
    XZ = sb("XZ", [128, 8, NTOK], F32)
    AR = sb("AR", [128, ARENA_B // 4], F32)
    WAR = sb("WAR", [128, WAR_B // 4], F32)
    T = [sb(f"T{i}", [128, 512], F32) for i in range(3)]
    SQ = [sb(f"SQ{i}", [128, 512], BF16) for i in range(2)]
    RS = sb("RS", [128, 512], F32)
    RSTD = RS
    ONES = sb("ONES", [128, 128], BF16)
    ID2 = sb("ID2", [2, 2], F32)
    EPSC = sb("EPSC", [128, 1], F32)
    IDB = sb("IDB", [128, 128], BF16)
    CC = sb("CC", [128, 8, 2], F32)
    SCB = sb("SCB", [128, 8, 2], BF16)
    MROW = SQ[1][0:2, :].bitcast(F32)
    MODT2 = [sb(f"MODT{i}", [128, 72, 2], F32) for i in range(2)]
    BMOD2 = [sb(f"BMOD{i}", [128, 72], F32) for i in range(2)]
    GT2 = [sb(f"GT{i}", [128, 6, 8], F32) for i in range(2)]
    DER2 = [sb(f"DER{i}", [128, 6, 8, 2], F32) for i in range(2)]
    cur = {"p": 0}
    CW = sb("CW", [128, 3, 8], F32)
    ESINK = sb("ESINK", [128, 16], F32)
    PS = [es.enter_context(nc.psum_tensor(f"PS{i}", [128, 512], F32)) for i in range(8)]

    def arena(off_b, nbytes, d, pattern=None, **kw):
        v = AR[:, off_b // 4:(off_b + nbytes) // 4]
        if d == BF16:
            v = v.bitcast(BF16)
        if pattern:
            v = v.rearrange(pattern, **kw)
        return v

    def war(off_b, nbytes, d, pattern=None, **kw):
        v = WAR[:, off_b // 4:(off_b + nbytes) // 4]
        if d == BF16:
            v = v.bitcast(BF16)
        if pattern:
            v = v.rearrange(pattern, **kw)
        return v

    psk = lambda b: ("ps", b)

    P.op("dve", lambda e: e.memset(ONES[:], 1.0), w=[("ones",)])
    P.op("dve", lambda e: e.memset(EPSC[:], EPS), w=[("eps",)])
    IDENT = AR[:, 2048:2176]
    P.dma("sp", lambda e: e.dma_start(out=IDENT, in_=id_d), "c0", w=[("ident",)])
    P.dma("sp", lambda e: e.dma_start(out=ID2[:], in_=id_d[0:2, 0:2]), "c9", w=[("id2",)])
    P.dma("sp", lambda e: e.dma_start(out=CC[:], in_=cc_d), "c1", w=[("cc",)])
    P.op("dve", lambda e: e.tensor_copy(IDB[:], IDENT), r=[("ident",)], w=[("idb",)])
    P.op("act", lambda e: e.activation(SCB[:], CC[:], AF.Silu), r=[("cc",)], w=[("scb",)])

    ST = [arena(i * 4096, 4096, F32) for i in range(2)]
    for blk in range(NTOK // 128):
        s = blk % 2
        tile_i = min(blk // 4, 4)
        P.dma("sp", lambda e, s=s, blk=blk: e.dma_start(out=ST[s], in_=xz_d[blk * 128:(blk + 1) * 128, :]),
              f"st{s}", w=[("st", s)])
        for hb in range(2):
            bank = hb + 2 * (blk % 2)

            def tr(e, s=s, hb=hb, bank=bank):
                ins = None
                for cc in range(4):
                    c = hb * 4 + cc
                    ins = e.matmul(PS[bank][:, cc * 128:(cc + 1) * 128], ST[s][:, c * 128:(c + 1) * 128],
                                   IDENT, start=True, stop=True)
                return ins
            P.op("pe", tr, r=[("st", s), ("ident",)], w=[psk(bank)])
            dst = XZ[:, hb * 4:(hb + 1) * 4, blk * 128:(blk + 1) * 128]
            src = PS[bank][:].rearrange("p (c n) -> p c n", c=4)
            eng = "act" if hb == 0 else "dve"
            if eng == "act":
                P.op("act", lambda e, dst=dst, src=src: e.activation(dst, src, AF.Copy), r=[psk(bank)],
                     w=[("xz", c, tile_i, blk % 4) for c in range(hb * 4, hb * 4 + 4)])
            else:
                P.op("dve", lambda e, dst=dst, src=src: e.tensor_copy(dst, src), r=[psk(bank)],
                     w=[("xz", c, tile_i, blk % 4) for c in range(hb * 4, hb * 4 + 4)])

    def xzk(c, ti):
        return [("xz", c, ti, q) for q in range(4)]

    def norm_in(li, tiles, Aidx, Bmod, hview, hkey, extra_w=(), sbanks=(6,)):
        p_ = cur["p"]
        MODT, DER = MODT2[p_], DER2[p_]
        nbk = len(sbanks)
        RSb = [(RS, ("rs",)), (T[2], ("T", 2))]

        def sq_ops(i):
            ti = tiles[i]
            s0, n = TILES[ti]
            sb = sbanks[i % nbk]
            rsb, rkey = RSb[i % nbk]
            th = []
            for c in range(8):
                th.append(lambda c=c: P.op("act", lambda e: e.activation(SQ[c % 2][:, :n], XZ[:, c, s0:s0 + n], AF.Square),
                                           r=xzk(c, ti), w=[("sq", c % 2)]))
                th.append(lambda c=c: P.op("pe", lambda e: e.matmul(PS[sb][:, :n], ONES[:], SQ[c % 2][:, :n], start=(c == 0), stop=(c == 7)),
                                           r=[("sq", c % 2), ("ones",)], w=[psk(sb)]))
            th.append(lambda: P.op("act", lambda e: e.activation(rsb[:, :n], PS[sb][:, :n], AF.Ln, bias=EPSC[:, 0:1], scale=1.0 / D),
                                   r=[psk(sb), ("eps",)], w=[rkey]))
            th.append(lambda: P.op("act", lambda e: e.activation(rsb[:, :n], rsb[:, :n], AF.Exp, scale=-0.5), r=[rkey], w=[rkey]))
            return th

        def ap_ops(i):
            ti = tiles[i]
            s0, n = TILES[ti]
            r = 1 if ti == 4 else 0
            rsb, rkey = RSb[i % nbk]
            hv = hview(ti)
            nt = 3 if nbk == 1 else 2
            th = []
            for c in range(8):
                th.append(lambda c=c: P.op("dve", lambda e: e.tensor_tensor(T[c % nt][:, :n], XZ[:, c, s0:s0 + n], rsb[:, :n], ALU.mult),
                                           r=xzk(c, ti) + [rkey], w=[("T", c % nt)]))
                th.append(lambda c=c: P.op("act", lambda e: e.activation(
                    hv[:, c, :], T[c % nt][:, :n], AF.Identity,
                    bias=MODT[:, Bmod * 8 + c, r:r + 1], scale=DER[:, Aidx, c, r:r + 1]),
                    r=[("T", c % nt), ("modt", p_), ("der", p_)],
                    w=[(hkey, c, ti)] + (list(extra_w) if (i == 0 and c == 0) else [])))
            return th

        if nbk == 1:
            for i in range(len(tiles)):
                for f_ in sq_ops(i) + ap_ops(i):
                    f_()
            return
        for f_ in sq_ops(0):
            f_()
        for i in range(len(tiles)):
            nxt = sq_ops(i + 1) if i + 1 < len(tiles) else []
            ap = ap_ops(i)
            while ap or nxt:
                for _ in range(2):
                    if nxt:
                        nxt.pop(0)()
                for _ in range(2):
                    if ap:
                        ap.pop(0)()

    def evac_stats(pb, n, ydst, ykey, Cidx, c, r, statbank, sqi):
        p_ = cur["p"]
        DER = DER2[p_]
        P.op("act", lambda e: e.activation(ydst, PS[pb][:, :n], AF.Identity, scale=DER[:, Cidx, c, r:r + 1]),
             r=[psk(pb), ("der", p_)], w=[ykey])
        P.op("act", lambda e: e.activation(SQ[sqi][:, :n], PS[pb][:, :n], AF.Square),
             r=[psk(pb)], w=[("sq", sqi)])
        return lambda: P.op("pe", lambda e: e.matmul(PS[statbank][:, :n], ONES[:], SQ[sqi][:, :n], start=(c == 0), stop=(c == 7)),
                            r=[("sq", sqi), ("ones",)], w=[psk(statbank)])

    def finish_ops(ti, yv, ykey, statbank, use_pool=True):
        s0, n = TILES[ti]
        th = []
        th.append(lambda: P.op("act", lambda e: e.activation(RS[:, :n], PS[statbank][:, :n], AF.Ln, bias=EPSC[:, 0:1], scale=1.0 / D),
                               r=[psk(statbank), ("eps",)], w=[("rs",)]))
        th.append(lambda: P.op("act", lambda e: e.activation(RS[:, :n], RS[:, :n], AF.Exp, scale=-0.5), r=[("rs",)], w=[("rs",)]))
        for c in range(8):
            th.append(lambda c=c: P.op("dve", lambda e: e.tensor_tensor(T[c % 2][:, :n], yv[:, c, :], RS[:, :n], ALU.mult),
                                       r=[ykey(c, ti), ("rs",)], w=[("T", c % 2)]))
            th.append(lambda c=c: P.op("pool" if (use_pool and c % 4 != 3) else "dve",
                                       lambda e: e.tensor_tensor(XZ[:, c, s0:s0 + n], T[c % 2][:, :n], XZ[:, c, s0:s0 + n], ALU.add),
                                       r=[("T", c % 2)] + xzk(c, ti), w=xzk(c, ti)))
        return th

    def finish_out(ti, yv, ykey, statbank):
        for f_ in finish_ops(ti, yv, ykey, statbank):
            f_()

    def mod_setup(li):
        p_ = li % 2
        P.dma("sp", lambda e: e.dma_start(out=BMOD2[p_][:], in_=bmod_d[li]), f"c2{p_}", w=[("bmod", p_)])
        P.dma("sp", lambda e: e.dma_start(out=GT2[p_][:], in_=g_d[li]), f"c3{p_}", w=[("gt", p_)])

    NMC = 36

    def mod_slot(s):
        if s < 2:
            return war(16384 + s * 5632, 4096, BF16, "p (k n) -> p k n", k=8)
        return war((s - 2) * 8192, 4096, BF16, "p (k n) -> p k n", k=8)

    def mod_keys(s):
        if s < 2:
            return [("wd", s)]
        return [("wg", s - 2, 0), ("wg", s - 2, 1)]

    def mod_dma(li, nb, nslots=2):
        s = nb % nslots
        slot = mod_slot(s)
        wsrc = wmod_d[li].rearrange("(k p) n -> p k n", p=128)
        P.dma("pool", lambda e: e.dma_start(out=slot, in_=wsrc[:, :, nb * 256:(nb + 1) * 256]),
              f"wm{s}", w=mod_keys(s))

    def mod_pe(li, nb, nslots=2):
        p_ = li % 2
        s = nb % nslots
        slot = mod_slot(s)

        def mm(e):
            ins = None
            for kc in range(8):
                ins = e.matmul(PS[7][0:2, 0:256], SCB[:, kc, :], slot[:, kc, :], start=(kc == 0), stop=(kc == 7))
            return ins
        P.op("pe", mm, r=mod_keys(s) + [("scb",)], w=[psk(7)])
        P.op("dve", lambda e: e.tensor_copy(MROW, PS[7][0:2, 0:256]), r=[psk(7)], w=[("sq", 1)])

        def trm(e):
            ins = None
            for j in range(2):
                ins = e.matmul(PS[7][:, 384 + 2 * j:384 + 2 * j + 2], MROW[0:2, j * 128:(j + 1) * 128], ID2[0:2, 0:2],
                               start=True, stop=True)
            return ins
        P.op("pe", trm, r=[("sq", 1), ("id2",)], w=[psk(7)])
        psv = PS[7][:, 384:388].rearrange("p (m r) -> p m r", r=2)
        for r in range(2):
            P.op("dve", lambda e, r=r: e.tensor_tensor(MODT2[p_][:, nb * 2:(nb + 1) * 2, r], psv[:, :, r],
                                                       BMOD2[p_][:, nb * 2:(nb + 1) * 2], ALU.add),
                 r=[psk(7), ("bmod", p_)], w=[("modt", p_, nb, r)])

    def mod_finish(li):
        p_ = li % 2
        MODT, GT, DER = MODT2[p_], GT2[p_], DER2[p_]
        allk = [("modt", p_, nb, r) for nb in range(NMC) for r in range(2)]
        spec = [(0, 1, 0, 1.0, "A"), (1, 2, 1, 0.5, "C"), (2, 4, 2, 1.0, "A"), (3, 5, 3, 1.0, "C"),
                (4, 7, 4, 1.0, "A"), (5, 8, 5, 0.5, "C")]
        first = True
        for (di, m, gi, wgt, kind) in spec:
            for r in range(2):
                rk = allk + [("gt", p_)] if first else [("modt", p_), ("gt", p_)]
                wk = [("der", p_), ("modt", p_)] if first else [("der", p_)]
                first = False
                if kind == "A":
                    P.op("dve", lambda e, di=di, m=m, gi=gi, r=r: e.scalar_tensor_tensor(
                        DER[:, di, :, r], MODT[:, m * 8:(m + 1) * 8, r], 1.0, GT[:, gi, :], ALU.add, ALU.mult),
                        r=rk, w=wk)
                else:
                    P.op("dve", lambda e, di=di, m=m, gi=gi, r=r, wgt=wgt: e.scalar_tensor_tensor(
                        DER[:, di, :, r], MODT[:, m * 8:(m + 1) * 8, r], wgt, GT[:, gi, :], ALU.mult, ALU.mult),
                        r=rk, w=wk)

    def mod_step(q):
        if q is None:
            return
        if q["pend"] is not None:
            mod_pe(q["li"], q["pend"])
            q["pend"] = None
        if q["next"] < NMC:
            mod_dma(q["li"], q["next"])
            q["pend"] = q["next"]
            q["next"] += 1

    def mod_flush_pending(q):
        if q is not None and q["pend"] is not None:
            mod_pe(q["li"], q["pend"])
            q["pend"] = None

    def ffn(li, si, Aidx, Bmod, Cidx, skip_z=False, modq=None):
        wg_src = wgu_d[li, si].rearrange("(k p) (g n) -> p k g n", p=128, g=2)
        wd_src = wdn_d[li, si].rearrange("(f p) d -> p f d", p=128)
        wgs = [war(i * 8192, 8192, BF16, "p (k g n) -> p k g n", k=8, g=2) for i in range(2)]
        wds = [war(16384 + i * 5632, 5632, BF16, "p (f n) -> p f n", f=NF) for i in range(2)]
        def group(gi, tiles, carry):
            loc = {}
            o = 0
            for t in tiles:
                loc[t] = o
                o += TILES[t][1]
            if gi == 0:
                ntok = 1280
                A_ = arena(0, NF * ntok * 2, BF16, "p (f n) -> p f n", f=NF)
                Y_ = arena(NF * ntok * 2, 8 * ntok * 4, F32, "p (c n) -> p c n", c=8)
                H_ = arena(NF * ntok * 2, 8 * ntok * 2, BF16, "p (c n) -> p c n", c=8)
            else:
                ntok = 1024
                H_ = arena(0, 8 * ntok * 2, BF16, "p (c n) -> p c n", c=8)
                A_ = arena(16384, NF * ntok * 2, BF16, "p (f n) -> p f n", f=NF)
                Y_ = arena(61440, 8 * ntok * 4, F32, "p (c n) -> p c n", c=8)
            hview = lambda ti: H_[:, :, loc[ti]:loc[ti] + TILES[ti][1]]
            yview = lambda ti: Y_[:, :, loc[ti]:loc[ti] + TILES[ti][1]]
            prev_keys, deferred = carry
            norm_in(li, tiles, Aidx, Bmod, hview, "h", extra_w=prev_keys, sbanks=((7, 6) if gi == 0 else (7,)))
            step = 0
            for fg in range(NF // 2):
                s = fg % 2
                for gu in range(2):
                    P.dma("pool", lambda e, s=s, fg=fg, gu=gu: e.dma_start(
                        out=wgs[s][:, :, gu, :], in_=wg_src[:, :, gu, fg * 256:(fg + 1) * 256]),
                        f"wg{s}{gu}", w=[("wg", s, gu)])
                if fg < 9:
                    mod_step(modq)
                elif fg == 9:
                    mod_flush_pending(modq)
                for j in range(2):
                    f = fg * 2 + j
                    if f == 18:
                        while deferred:
                            deferred.pop(0)()
                    for t in tiles:
                        n = TILES[t][1]
                        pb = (0, 4)[step % 2]
                        step += 1

                        def mmg(e, s=s, j=j, t=t, n=n, pb=pb, gu=0):
                            ins = None
                            for kc in range(8):
                                ins = e.matmul(PS[pb + gu][:, :n], wgs[s][:, kc, gu, j * 128:(j + 1) * 128],
                                               H_[:, kc, loc[t]:loc[t] + n], start=(kc == 0), stop=(kc == 7))
                            return ins
                        hk = [("h", c, t) for c in range(8)]
                        P.op("pe", mmg, r=[("wg", s, 0)] + hk, w=[psk(pb)])
                        P.op("pe", lambda e, f=mmg: f(e, gu=1), r=[("wg", s, 1)] + hk, w=[psk(pb + 1)])
                        P.op("act", lambda e, n=n, pb=pb: e.activation(T[2][:, :n], PS[pb][:, :n], AF.Silu),
                             r=[psk(pb)], w=[("T", 2)])
                        P.op("dve", lambda e, f=f, t=t, n=n, pb=pb: e.tensor_tensor(
                            A_[:, f, loc[t]:loc[t] + n], T[2][:, :n], PS[pb + 1][:, :n], ALU.mult),
                            r=[("T", 2), psk(pb + 1)], w=[("a", f, t)])
                        for _ in range(2):
                            if deferred:
                                deferred.pop(0)()
            while deferred:
                deferred.pop(0)()
            mod_flush_pending(modq)
            SB3 = [2, 3, 6]
            pend = None
            step = 0
            for dc in range(8):
                s = dc % 2
                P.dma("pool", lambda e, s=s, dc=dc: e.dma_start(out=wds[s], in_=wd_src[:, :, dc * 128:(dc + 1) * 128]),
                      f"wd{s}", w=[("wd", s)])
                for k, t in enumerate(tiles):
                    n = TILES[t][1]
                    pb = 4 + (step % 2)

                    def mmd(e, s=s, t=t, n=n, pb=pb):
                        ins = None
                        for fc in range(NF):
                            ins = e.matmul(PS[pb][:, :n], wds[s][:, fc, :], A_[:, fc, loc[t]:loc[t] + n],
                                           start=(fc == 0), stop=(fc == NF - 1))
                        return ins
                    P.op("pe", mmd, r=[("wd", s)] + [("a", f, t) for f in range(NF)], w=[psk(pb)])
                    nxt = evac_stats(pb, n, Y_[:, dc, loc[t]:loc[t] + n], ("y", dc, t), Cidx, dc, 1 if t == 4 else 0,
                                     SB3[k], step % 2)
                    step += 1
                    if pend is not None:
                        pend()
                    pend = nxt
            pend()
            fin = []
            for k, t in enumerate(tiles):
                fin += finish_ops(t, yview(t), lambda c, ti: ("y", c, ti), SB3[k], use_pool=(gi == len(GROUPS) - 1))
            akeys = [("a", f, t) for f in range(NF) for t in tiles]
            return (akeys, fin)

        carry = ([], [])
        for gi, grp in enumerate(GROUPS):
            carry = group(gi, [t for t in grp if not (skip_z and t == 4)], carry)
        for f_ in carry[1]:
            f_()
        P.fence()

    def conv_mixer(li, j, update_z=True):
        HX = arena(0, 36864, BF16, "p (c n) -> p c n", c=8)
        VB = arena(36864, 36864, BF16, "p (c n) -> p c n", c=8)
        CU = arena(73728, 9232, F32)
        BB = arena(82960, 9232, F32)
        WO = arena(0, 16384, BF16, "p (k n) -> p k n", k=8)
        YT = arena(16384, 16384, F32, "p (c n) -> p c n", c=8)
        wis = [war(i * 6144, 6144, BF16, "p (k g n) -> p k g n", k=8, g=3) for i in range(2)]
        tiles = [0, 1, 2, 3, 4]
        P.dma("sp", lambda e: e.dma_start(out=CW[:], in_=cw_d[j]), "c4", w=[("cw",)])
        norm_in(li, tiles, 2, 3, lambda ti: HX[:, :, TILES[ti][0]:TILES[ti][0] + TILES[ti][1]], "h", sbanks=(6, 7))
        P.op("dve", lambda e: e.memset(CU[:], 0.0), w=[("cu", t) for t in tiles] + [("cupad",)])
        yi = lambda t: TILES[t][0] if t < 4 else 2050
        win_src = cwin_d[j].rearrange("(k p) (g n) -> p k g n", p=128, g=3)
        for fc in range(8):
            s = fc % 2
            for g3 in range(3):
                P.dma("pool", lambda e, s=s, fc=fc, g3=g3: e.dma_start(
                    out=wis[s][:, :, g3, :], in_=win_src[:, :, g3, fc * 128:(fc + 1) * 128]),
                    f"wi{s}{g3}", w=[("wi", s, g3)])
            for t in tiles:
                s0, n = TILES[t]
                pb = 3 * ((fc * 5 + t) % 2)
                hk = [("h", c, t) for c in range(8)]
                for g3 in range(3):
                    def mmc(e, s=s, g3=g3, s0=s0, n=n, pb=pb):
                        ins = None
                        for kc in range(8):
                            ins = e.matmul(PS[pb + g3][:, :n], wis[s][:, kc, g3, :], HX[:, kc, s0:s0 + n],
                                           start=(kc == 0), stop=(kc == 7))
                        return ins
                    P.op("pe", mmc, r=[("wi", s, g3)] + hk, w=[psk(pb + g3)])
                y0 = yi(t)
                P.op("act", lambda e, y0=y0, n=n, pb=pb: e.activation(BB[:, y0:y0 + n], PS[pb][:, :n], AF.Copy),
                     r=[psk(pb)], w=[("bb", t)])
                P.op("act", lambda e, y0=y0, n=n, pb=pb: e.activation(CU[:, y0 + 1:y0 + 1 + n], PS[pb + 1][:, :n], AF.Copy),
                     r=[psk(pb + 1)], w=[("cu", t)])
                P.op("dve", lambda e, y0=y0, n=n, pb=pb: e.tensor_tensor(
                    CU[:, y0 + 1:y0 + 1 + n], CU[:, y0 + 1:y0 + 1 + n], PS[pb + 2][:, :n], ALU.mult),
                    r=[("cu", t), psk(pb + 2)], w=[("cu", t)])
            for t in tiles:
                s0, n = TILES[t]
                y0 = yi(t)
                nb = [("cu", t), ("cupad",), ("cw",)]
                if 0 < t < 4:
                    nb.append(("cu", t - 1))
                if t < 3:
                    nb.append(("cu", t + 1))
                P.op("act", lambda e, fc=fc, y0=y0, n=n: e.activation(T[0][:, :n], CU[:, y0 + 1:y0 + 1 + n], AF.Identity,
                                                                   scale=CW[:, 1, fc:fc + 1]),
                     r=nb, w=[("T", 0)])
                P.op("act", lambda e, fc=fc, y0=y0, n=n: e.activation(T[1][:, :n], CU[:, y0:y0 + n], AF.Identity,
                                                                   scale=CW[:, 0, fc:fc + 1]),
                     r=nb, w=[("T", 1)])
                P.op("act", lambda e, fc=fc, y0=y0, n=n: e.activation(T[2][:, :n], CU[:, y0 + 2:y0 + 2 + n], AF.Identity,
                                                                   scale=CW[:, 2, fc:fc + 1]),
                     r=nb, w=[("T", 2)])
                P.op("dve", lambda e, n=n: e.tensor_tensor(T[0][:, :n], T[0][:, :n], T[1][:, :n], ALU.add),
                     r=[("T", 0), ("T", 1)], w=[("T", 0)])
                P.op("dve", lambda e, n=n: e.tensor_tensor(T[0][:, :n], T[0][:, :n], T[2][:, :n], ALU.add),
                     r=[("T", 0), ("T", 2)], w=[("T", 0)])
                P.op("dve", lambda e, fc=fc, s0=s0, y0=y0, n=n: e.tensor_tensor(VB[:, fc, s0:s0 + n], T[0][:, :n], BB[:, y0:y0 + n], ALU.mult),
                     r=[("T", 0), ("bb", t)], w=[("v", fc, t)])
        P.fence()
        P.dma("pool", lambda e: e.dma_start(out=WO, in_=cwout_d[j].rearrange("(k p) d -> p k d", p=128)), "wo", w=[("wo",)])
        YTs = [YT, arena(73728, 16384, F32, "p (c n) -> p c n", c=8)]
        for t in (tiles if update_z else tiles[:4]):
            s0, n = TILES[t]
            YT = YTs[t % 2]
            ytn = "yt%d" % (t % 2)
            pend = None
            for dc in range(8):
                pb = dc % 2

                def mmo(e, dc=dc, s0=s0, n=n, pb=pb):
                    ins = None
                    for fc in range(8):
                        ins = e.matmul(PS[pb][:, :n], WO[:, fc, dc * 128:(dc + 1) * 128], VB[:, fc, s0:s0 + n],
                                       start=(fc == 0), stop=(fc == 7))
                    return ins
                P.op("pe", mmo, r=[("wo",)] + [("v", fc, t) for fc in range(8)], w=[psk(pb)])
                nxt = evac_stats(pb, n, YT[:, dc, :n], (ytn, dc), 3, dc, 1 if t == 4 else 0, 6, dc % 2)
                if pend is not None:
                    pend()
                pend = nxt
            pend()
            finish_out(t, YT[:, :, :n], lambda c, ti, ytn=ytn: (ytn, c), 6)
        P.fence()

    def attn_mixer(li, j, ctx_queries):
        HX = arena(0, 36864, BF16, "p (c n) -> p c n", c=8)
        COS = arena(36864, 9216, F32)
        SIN = arena(46080, 9216, F32)
        KT = arena(55296, 9216, BF16, "p (c n) -> p c n", c=2)
        VV = arena(64512, 9216, BF16, "p (b n) -> p b n", b=18)
        QR = arena(73728, 8192, BF16, "p (c n) -> p c n", c=8)
        OC = arena(81920, 8192, BF16, "p (c n) -> p c n", c=8)
        PT = [arena(90112 + i * 1024, 1024, BF16) for i in range(3)]
        wsl = [war(i * 4096, 4096, BF16, "p (k g n) -> p k g n", k=8, g=2) for i in range(2)]
        YT = war(8192, 16384, F32, "p (c n) -> p c n", c=8)
        MK = war(24576, 512, BF16, "p (m n) -> p m n", m=2)
        tiles = [0, 1, 2, 3, 4]
        P.dma("sp", lambda e: e.dma_start(out=COS, in_=cos_d), "c5", w=[("cos",)])
        P.dma("sp", lambda e: e.dma_start(out=SIN, in_=sin_d), "c6", w=[("sin",)])
        P.dma("pool", lambda e: e.dma_start(out=MK, in_=mask_d), "c7", w=[("mk",)])
        P.dma("sp", lambda e: e.dma_start(out=ESINK[:], in_=sink_d[j]), "c8", w=[("esink",)])
        P.op("act", lambda e: e.activation(ESINK[:], ESINK[:], AF.Exp), r=[("esink",)], w=[("esink",)])
        norm_in(li, tiles, 2, 3, lambda ti: HX[:, :, TILES[ti][0]:TILES[ti][0] + TILES[ti][1]], "h", sbanks=(6, 7))
        wsrc = wqk_d[j].rearrange("(k p) c g n -> p k c g n", p=128)

        def load_qk(c, s):
            for g2 in range(2):
                P.dma("pool", lambda e, c=c, s=s, g2=g2: e.dma_start(out=wsl[s][:, :, g2, :], in_=wsrc[:, :, c, g2, :]),
                      f"wq{s}{g2}", w=[("wq", s, g2)])

        def proj_rope(c_w, s, t, dst, dkey, pb=4):
            s0, n = TILES[t]
            hk = [("h", c, t) for c in range(8)]
            for g2 in range(2):
                def mmq(e, g2=g2, s=s, s0=s0, n=n):
                    ins = None
                    for kc in range(8):
                        ins = e.matmul(PS[pb + g2][:, :n], wsl[s][:, kc, g2, :], HX[:, kc, s0:s0 + n],
                                       start=(kc == 0), stop=(kc == 7))
                    return ins
                P.op("pe", mmq, r=[("wq", s, g2)] + hk, w=[psk(pb + g2)])
            P.op("dve", lambda e, s0=s0, n=n: e.tensor_tensor(T[0][:, :n], PS[pb][:, :n], COS[:, s0:s0 + n], ALU.mult),
                 r=[psk(pb), ("cos",)], w=[("T", 0)])
            P.op("dve", lambda e, s0=s0, n=n: e.tensor_tensor(T[1][:, :n], PS[pb + 1][:, :n], SIN[:, s0:s0 + n], ALU.mult),
                 r=[psk(pb + 1), ("sin",)], w=[("T", 1)])
            P.op("dve", lambda e, n=n, dst=dst: e.tensor_tensor(dst, T[0][:, :n], T[1][:, :n], ALU.add),
                 r=[("T", 0), ("T", 1)], w=[dkey])

        for c2 in range(2):
            load_qk(8 + c2, c2 % 2)
            for t in tiles:
                s0, n = TILES[t]
                proj_rope(8 + c2, c2 % 2, t, KT[:, c2, s0:s0 + n], ("kt", c2, t), pb=(4, 0)[t % 2])
        WV = war(0, 4096, BF16, "p (k n) -> p k n", k=8)
        P.dma("pool", lambda e: e.dma_start(out=WV, in_=wv_d[j].rearrange("(k p) n -> p k n", p=128)), "wvv",
              w=[("wq", 0, 0), ("wq", 0, 1)])
        for blk in range(18):
            t = min(blk // 4, 4)
            pb = blk % 2

            def mmv(e, blk=blk, pb=pb):
                ins = None
                for kc in range(8):
                    ins = e.matmul(PS[pb][:, 0:256], HX[:, kc, blk * 128:(blk + 1) * 128], WV[:, kc, :],
                                   start=(kc == 0), stop=(kc == 7))
                return ins
            P.op("pe", mmv, r=[("wq", 0, 0), ("wq", 0, 1)] + [("h", c, t) for c in range(8)], w=[psk(pb)])
            P.op("act", lambda e, blk=blk, pb=pb: e.activation(VV[:, blk, :], PS[pb][:, 0:256], AF.Copy),
                 r=[psk(pb)], w=[("vv", blk)])
        wo_src = wo_d[j].rearrange("(k p) d -> p k d", p=128)
        qtiles = tiles if ctx_queries else tiles[:4]
        pti = 0
        for t in qtiles:
            s0, n = TILES[t]
            for c in range(8):
                load_qk(c, c % 2)
                proj_rope(c, c % 2, t, QR[:, c, :n], ("qr", c), pb=(4, 0)[c % 2])
            n0 = s0 // 128
            steps = []
            for c in range(8):
                for half in range(2):
                    h = (HE if half == 0 else HO)[c]
                    g = h // 4
                    kbs = [(16, 0, n), (17, 0, n)]
                    if t < 4:
                        for m in range(max(0, n0 - 1), min(15, n0 + 4) + 1):
                            lo = max(m - 1, n0)
                            hi = min(m + 1, n0 + 3)
                            kbs.append((m, (lo - n0) * 128, (hi - n0 + 1) * 128))
                    for ki, (kb, qa, qb) in enumerate(kbs):
                        steps.append(dict(c=c, half=half, h=h, kc2=g // 2, kb=kb, qa=qa, qb=qb,
                                          first=(ki == 0), last=(ki == len(kbs) - 1)))
            SBK = [4, 5, 7]

            def emit_S(i):
                st = steps[i]
                sbk = SBK[i % 3]
                Pp = slice(st["half"] * 64, st["half"] * 64 + 64)
                kb, qa, qb, c, kc2 = st["kb"], st["qa"], st["qb"], st["c"], st["kc2"]
                mk = []
                if kb < 16:
                    if n0 <= kb - 1 <= n0 + 3:
                        mk.append(((kb - 1 - n0) * 128, 0))
                    if n0 <= kb + 1 <= n0 + 3:
                        mk.append(((kb + 1 - n0) * 128, 1))

                def f(e, kb=kb, qa=qa, qb=qb, sbk=sbk, Pp=Pp, kc2=kc2, c=c, mk=mk):
                    ins = e.matmul(PS[sbk][:, qa:qb], KT[Pp, kc2, kb * 128:(kb + 1) * 128], QR[Pp, c, qa:qb],
                                   start=True, stop=(len(mk) == 0))
                    for mi, (a0, which) in enumerate(mk):
                        ins = e.matmul(PS[sbk][:, a0:a0 + 128], IDB[:], MK[:, which, :], start=False,
                                       stop=(mi == len(mk) - 1))
                    return ins
                P.op("pe", f, r=[("kt", kc2, min(kb // 4, 4)), ("qr", c), ("mk",), ("idb",)], w=[psk(sbk)])

            def emit_exp(i):
                st = steps[i]
                sbk = SBK[i % 3]
                pt = PT[i % 3]
                P.op("act", lambda e, qa=st["qa"], qb=st["qb"], sbk=sbk, pt=pt: e.activation(
                    pt[:, qa:qb], PS[sbk][:, qa:qb], AF.Exp, scale=0.125),
                    r=[psk(sbk)], w=[("pt", i % 3)])

            def emit_PV(i):
                st = steps[i]
                pt = PT[i % 3]
                ptk = ("pt", i % 3)
                half, c, h, kb, qa, qb, kc2 = st["half"], st["c"], st["h"], st["kb"], st["qa"], st["qb"], st["kc2"]
                first, last = st["first"], st["last"]
                ob = 2 * half
                Pp = slice(half * 64, half * 64 + 64)
                P.op("pe", lambda e, kb=kb, qa=qa, qb=qb, ob=ob, pt=pt, kc2=kc2, first=first, last=last: e.matmul(
                    PS[ob][:, qa:qb], VV[:, kb, kc2 * 128:(kc2 + 1) * 128], pt[:, qa:qb],
                    start=first, stop=last),
                    r=[ptk, ("vv", kb)], w=[psk(ob)])
                P.op("pe", lambda e, qa=qa, qb=qb, ob=ob, pt=pt, first=first, last=last: e.matmul(
                    PS[ob + 1][:, qa:qb], ONES[:], pt[:, qa:qb], start=first, stop=last),
                    r=[ptk, ("ones",)], w=[psk(ob + 1)])
                if last:
                    normq.append((i + 3, lambda Pp=Pp, ob=ob, h=h, half=half, c=c: emit_norm(Pp, ob, h, half, c)))

            def emit_norm(Pp, ob, h, half, c):
                if True:
                    tk = ("T", half)
                    P.op("act", lambda e, Pp=Pp, n=n, ob=ob, h=h, half=half: e.activation(
                        T[half][Pp, :n], PS[ob + 1][Pp, :n], AF.Ln, bias=ESINK[Pp, h:h + 1], scale=1.0),
                        r=[psk(ob + 1), ("esink",)], w=[tk])
                    P.op("act", lambda e, Pp=Pp, n=n, half=half: e.activation(T[half][Pp, :n], T[half][Pp, :n], AF.Exp, scale=-1.0),
                         r=[tk], w=[tk])
                    P.op("dve", lambda e, Pp=Pp, n=n, ob=ob, c=c, half=half: e.tensor_tensor(
                        OC[Pp, c, :n], PS[ob][Pp, :n], T[half][Pp, :n], ALU.mult),
                        r=[psk(ob), tk], w=[("oc", c, half)])

            LOOK = 2
            ns = len(steps)
            normq = []
            for i in range(min(LOOK, ns)):
                emit_S(i)
            for i in range(ns):
                emit_exp(i)
                while normq and normq[0][0] <= i:
                    normq.pop(0)[1]()
                if i + LOOK < ns:
                    emit_S(i + LOOK)
                emit_PV(i)
            while normq:
                normq.pop(0)[1]()
            pend = None
            for dc in range(8):
                s = (dc % 4) // 2
                g2 = dc % 2
                P.dma("pool", lambda e, s=s, g2=g2, dc=dc: e.dma_start(out=wsl[s][:, :, g2, :], in_=wo_src[:, :, dc * 128:(dc + 1) * 128]),
                      f"wq{s}{g2}", w=[("wq", s, g2)])
                pbo = 4 + dc % 2

                def mmo(e, s=s, g2=g2, n=n, pbo=pbo):
                    ins = None
                    for kc in range(8):
                        ins = e.matmul(PS[pbo][:, :n], wsl[s][:, kc, g2, :], OC[:, kc, :n], start=(kc == 0), stop=(kc == 7))
                    return ins
                P.op("pe", mmo, r=[("wq", s, g2)] + [("oc", c, hh) for c in range(8) for hh in range(2)], w=[psk(pbo)])
                nxt = evac_stats(pbo, n, YT[:, dc, :n], ("yt", dc), 3, dc, 1 if t == 4 else 0, 6, dc % 2)
                if pend is not None:
                    pend()
                pend = nxt
            pend()
            finish_out(t, YT[:, :, :n], lambda c, ti: ("yt", c), 6)
        P.fence()

    mod_setup(0)
    for nb in range(min(3, NMC)):
        mod_dma(0, nb, 4)
    for nb in range(NMC):
        if nb + 3 < NMC:
            mod_dma(0, nb + 3, 4)
        mod_pe(0, nb, 4)
    mod_finish(0)
    P.fence()
    for li in range(n_layers):
        last = li == DEPTH - 1
        use_attn = ((li % 2) == 1) or force_attn
        j = li // 2
        cur["p"] = li % 2
        q = None
        if li + 1 < n_layers:
            q = {"li": li + 1, "next": 0, "pend": None}
            mod_setup(li + 1)
        if stop_after == (li, "mod"):
            break
        ffn(li, 0, 0, 0, 1, modq=q)
        if stop_after == (li, "ffn0"):
            break
        if use_attn:
            attn_mixer(li, j, ctx_queries=not last)
        else:
            conv_mixer(li, j, update_z=not last)
        if stop_after == (li, "mix"):
            break
        ffn(li, 1, 4, 6, 5, skip_z=last, modq=q)
        if q is not None:
            while q["next"] < NMC or q["pend"] is not None:
                mod_step(q)
            mod_finish(li + 1)
            P.fence()

    P.fence()
    P.dma("sp", lambda e: e.dma_start(out=IDENT, in_=id_d), "c0", w=[("ident",)])
    OT = [arena(i * 4096, 4096, F32) for i in range(2)]
    for blk in range(NLAT // 128):
        s = blk % 2
        ti = blk // 4
        for hb in range(2):
            bank = hb + 2 * s

            def tro(e, hb=hb, bank=bank, blk=blk):
                ins = None
                for cc in range(4):
                    c = hb * 4 + cc
                    ins = e.matmul(PS[bank][:, cc * 128:(cc + 1) * 128], XZ[:, c, blk * 128:(blk + 1) * 128],
                                   IDENT, start=True, stop=True)
                return ins
            P.op("pe", tro, r=[("ident",)] + [k for c in range(hb * 4, hb * 4 + 4) for k in xzk(c, ti)], w=[psk(bank)])
            if hb == 0:
                P.op("act", lambda e, s=s, bank=bank: e.activation(OT[s][:, 0:512], PS[bank][:], AF.Copy),
                     r=[psk(bank)], w=[("ot", s, 0)])
            else:
                P.op("dve", lambda e, s=s, bank=bank: e.tensor_copy(OT[s][:, 512:1024], PS[bank][:]),
                     r=[psk(bank)], w=[("ot", s, 1)])
        P.dma("sp", lambda e, s=s, blk=blk: e.dma_start(out=out_d[blk * 128:(blk + 1) * 128, :], in_=OT[s]),
              f"o{s}", r=[("ot", s, 0), ("ot", s, 1)], w=[("out", blk)])
    P.op("sp", None, r=[("out", b) for b in range(NLAT // 128)])

    P.finalize()
    sems = {}
    for e in Prog.ENGS:
        sems[("e", e)] = es.enter_context(nc.semaphore(f"sem_{e}"))
    for d in P.dsems:
        sems[("d", d)] = es.enter_context(nc.semaphore(f"dsem_{d}"))
    with nc.Block() as block:
        P.emit(nc, block, sems)
    es.close()
    return nc, P


def _rope_tables():
    p = np.arange(128)
    d = p % 64
    a = d // 32
    half = (d % 32) // 16
    i = d % 16
    inv_freq = (10000.0 ** (-np.arange(16, dtype=np.float32) / 16)).astype(np.float32)
    tok = np.arange(NLAT)
    row = (tok // 64).astype(np.float32)
    col = (tok % 64).astype(np.float32)
    pos = np.where(a[:, None] == 0, row[None, :], col[None, :]).astype(np.float32)
    ang = (pos * inv_freq[i][:, None]).astype(np.float32)
    cos = np.cos(ang).astype(np.float32)
    sin = np.sin(ang).astype(np.float32)
    sgn = np.where(half == 0, -1.0, 1.0).astype(np.float32)[:, None]
    cosT = np.ones((128, NTOK), np.float32)
    sinT = np.zeros((128, NTOK), np.float32)
    cosT[:, :NLAT] = cos
    sinT[:, :NLAT] = sin * sgn
    return cosT, sinT


def _swap_cols(w):
    sh = w.shape
    w4 = w.reshape(sh[:-1] + (sh[-1] // 32, 2, 16))
    return np.ascontiguousarray(w4[..., ::-1, :]).reshape(sh)


def prepare_inputs(x, c, ctx, c_ctx, w_mod, b_mod, norm_g, ffn_w_gu, ffn_w_down,
                   conv_w_in, conv_w, conv_w_out, attn_w_qkv, attn_w_o, attn_sink, cores=None, n_layers=DEPTH):
    f = lambda a: np.ascontiguousarray(np.asarray(a, dtype=np.float32))
    x, c, ctx, c_ctx = f(x), f(c), f(ctx), f(c_ctx)
    w_mod, b_mod, norm_g = f(w_mod), f(b_mod), f(norm_g)
    attn_w_qkv, attn_w_o, attn_sink = f(attn_w_qkv), f(attn_w_o), f(attn_sink)
    conv_w = f(conv_w)
    B = x.shape[0]
    cosT, sinT = _rope_tables()
    kk = np.arange(128)
    masks = np.zeros((128, 2, 128), np.float32)
    masks[:, 0, :] = np.where(kk[:, None] <= kk[None, :], 0.0, -30000.0)
    masks[:, 1, :] = np.where(kk[:, None] >= kk[None, :], 0.0, -30000.0)
    ident = np.eye(128, dtype=np.float32)
    bmodT = f(b_mod.reshape(-1, 72, 128).transpose(0, 2, 1))
    gT = f(norm_g.reshape(-1, 6, 8, 128).transpose(0, 3, 1, 2))
    convwT = f(conv_w.reshape(-1, 3, 8, 128).transpose(0, 3, 1, 2))
    nA = attn_w_qkv.shape[0]
    wq = attn_w_qkv[:, :, :1024].reshape(nA, D, 16, 64)
    wk = attn_w_qkv[:, :, 1024:1280].reshape(nA, D, 4, 64)
    chunks = []
    for cch in range(8):
        chunks.append(np.concatenate([wq[:, :, HE[cch]], wq[:, :, HO[cch]]], axis=-1))
    for c2 in range(2):
        chunks.append(np.concatenate([wk[:, :, 2 * c2], wk[:, :, 2 * c2 + 1]], axis=-1))
    wqk = np.stack(chunks, axis=2)
    wqk = np.stack([wqk, _swap_cols(wqk)], axis=3)
    wv = f(attn_w_qkv[:, :, 1280:1536])
    rows = []
    for cch in range(8):
        for h in (HE[cch], HO[cch]):
            rows.append(np.arange(h * 64, h * 64 + 64))
    rows = np.concatenate(rows)
    wo = f(attn_w_o[:, rows, :])
    sinkB = f(np.broadcast_to(attn_sink[:, None, :], (nA, 128, 16)))
    shared = {
        "w_mod": w_mod, "bmodT": bmodT, "gT": gT, "ffn_w_gu": f(ffn_w_gu), "ffn_w_down": f(ffn_w_down),
        "conv_w_in": f(conv_w_in), "convwT": convwT, "conv_w_out": f(conv_w_out),
        "wqk": f(wqk), "wv": wv, "wo": wo, "sinkB": sinkB, "cosT": cosT, "sinT": sinT,
        "masks": masks, "ident": ident,
    }
    if n_layers != DEPTH:
        N2 = max(1, (n_layers + 1) // 2)
        for k in ("w_mod", "bmodT", "gT", "ffn_w_gu", "ffn_w_down"):
            shared[k] = np.ascontiguousarray(shared[k][:n_layers])
        for k in ("conv_w_in", "convwT", "conv_w_out", "wqk", "wv", "wo", "sinkB"):
            shared[k] = np.ascontiguousarray(shared[k][:N2])
    in_maps = []
    for b in (range(B) if cores is None else cores):
        m = dict(shared)
        m["xz"] = f(np.concatenate([x[b], ctx[b]], axis=0))
        cc2 = np.stack([c[b], c_ctx], axis=0)
        m["ccT"] = f(cc2.reshape(2, 8, 128).transpose(2, 1, 0))
        in_maps.append(m)
    return in_maps


_CACHE = {}


def kernel(x, c, ctx, c_ctx, w_mod, b_mod, norm_g, ffn_w_gu, ffn_w_down,
           conv_w_in, conv_w, conv_w_out, attn_w_qkv, attn_w_o, attn_sink):
    in_maps = prepare_inputs(x, c, ctx, c_ctx, w_mod, b_mod, norm_g, ffn_w_gu, ffn_w_down,
                             conv_w_in, conv_w, conv_w_out, attn_w_qkv, attn_w_o, attn_sink)
    if "nc" not in _CACHE:
        _CACHE["nc"] = build_program()[0]
    nc = _CACHE["nc"]
    res = run_bass_kernel_spmd(nc, in_maps, core_ids=list(range(len(in_maps))))
    out = np.stack([np.asarray(r["out"], dtype=np.float32) for r in res.results], axis=0)
    return out
```

```python
import numpy as np
import concourse.bass as bass
import concourse.mybir as mybir
from concourse.bass_utils import run_bass_kernel_spmd

F32, BF16 = mybir.dt.float32, mybir.dt.bfloat16
AF = mybir.ActivationFunctionType
ALU = mybir.AluOpType

D = 1024
NLAT = 2048
NZ = 256
NTOK = NLAT + NZ
DFF = 2816
NF = DFF // 128
DEPTH = 4
EPS = 1e-6
TILES = [(0, 512), (512, 512), (1024, 512), (1536, 512), (2048, 256)]
GROUPS = [[0, 1, 4], [2, 3]]
HE = [0, 1, 2, 3, 8, 9, 10, 11]
HO = [4, 5, 6, 7, 12, 13, 14, 15]
import os
DBG = os.environ.get('KDBG', '')
ARENA_B = 97280
WAR_B = 27648


class Op:
    __slots__ = ("eng", "fn", "reads", "writes", "dsem", "waits", "signal", "val", "deps", "fence")

    def __init__(self, eng, fn, reads, writes, dsem=None):
        self.eng = eng
        self.fn = fn
        self.reads = reads
        self.writes = writes
        self.dsem = dsem
        self.waits = []
        self.signal = False
        self.val = 0
        self.deps = ()
        self.fence = False


class Prog:
    ENGS = ("pe", "act", "dve", "pool", "sp")

    def __init__(self):
        self.ops = []
        self.pending_fence = False

    def op(self, eng, fn, r=(), w=()):
        o = Op(eng, fn, tuple(r), tuple(w))
        if self.pending_fence:
            pass
        self.ops.append(o)
        return o

    def dma(self, eng, fn, dsem, r=(), w=()):
        o = Op(eng, fn, tuple(r), tuple(w), dsem=dsem)
        self.ops.append(o)
        return o

    def fence(self):
        o = Op(None, None, (), ())
        o.fence = True
        self.ops.append(o)

    def finalize(self):
        ops = self.ops
        last_w = {}
        readers = {}
        last_on_eng = {}
        outstanding_dma = []
        fence_deps = None
        fenced = {e: True for e in self.ENGS}
        all_dma = []
        for i, o in enumerate(ops):
            if o.fence:
                fence_deps = set(last_on_eng.values()) | set(all_dma)
                all_dma = []
                fenced = {e: False for e in self.ENGS}
                continue
            deps = set()
            for k in o.reads:
                j = last_w.get(k)
                if j is not None:
                    deps.add(j)
            for k in o.writes:
                j = last_w.get(k)
                if j is not None:
                    deps.add(j)
                for j in readers.get(k, ()):
                    deps.add(j)
            if fence_deps is not None and not fenced[o.eng]:
                deps |= fence_deps
                fenced[o.eng] = True
            deps.discard(i)
            o.deps = deps
            for k in o.reads:
                readers.setdefault(k, []).append(i)
            for k in o.writes:
                last_w[k] = i
                readers[k] = []
            if o.dsem is None:
                last_on_eng[o.eng] = i
            else:
                all_dma.append(i)
        for i, o in enumerate(ops):
            if o.fence:
                continue
            for j in o.deps:
                p = ops[j]
                if p.dsem is not None:
                    p.signal = True
                elif p.eng == o.eng and p.eng in ("pe", "sp"):
                    continue
                else:
                    p.signal = True
        cnt = {}
        for o in ops:
            if o.fence:
                continue
            if o.dsem is not None:
                cnt[o.dsem] = cnt.get(o.dsem, 0) + 16
                o.val = cnt[o.dsem]
                o.signal = True
            elif o.signal:
                cnt[o.eng] = cnt.get(o.eng, 0) + 1
                o.val = cnt[o.eng]
        seen = {e: {} for e in self.ENGS}
        for o in ops:
            if o.fence:
                continue
            need = {}
            for j in o.deps:
                p = ops[j]
                if p.dsem is not None:
                    key = ("d", p.dsem)
                elif p.eng == o.eng and p.eng in ("pe", "sp"):
                    continue
                else:
                    key = ("e", p.eng)
                if p.val > need.get(key, 0):
                    need[key] = p.val
            sn = seen[o.eng]
            for key, v in need.items():
                if sn.get(key, 0) >= v:
                    continue
                sn[key] = v
                o.waits.append((key, v))
        self.dsems = sorted({o.dsem for o in ops if (not o.fence) and o.dsem is not None})
        self.maxcnt = cnt

    def emit(self, nc, block, sems):
        by_eng = {e: [] for e in self.ENGS}
        for o in self.ops:
            if not o.fence:
                by_eng[o.eng].append(o)

        def run(e, name):
            for o in by_eng[name]:
                for key, v in o.waits:
                    e.wait_ge(sems[key], v)
                if o.fn is None:
                    continue
                ins = o.fn(e)
                if o.signal:
                    if o.dsem is not None:
                        ins.then_inc(sems[("d", o.dsem)], 16)
                    else:
                        ins.then_inc(sems[("e", name)], 1)

        @block.tensor
        def _(e):
            run(e, "pe")

        @block.scalar
        def _(e):
            run(e, "act")

        @block.vector
        def _(e):
            run(e, "dve")

        @block.gpsimd
        def _(e):
            run(e, "pool")

        @block.sync
        def _(e):
            run(e, "sp")


def build_program(n_layers=DEPTH, stop_after=None, force_attn=False):
    nc = bass.Bass("TRN2", target_bir_lowering=False)
    dt = nc.dram_tensor
    xz_d = dt("xz", [NTOK, D], F32, kind="ExternalInput").ap()
    cc_d = dt("ccT", [128, 8, 2], F32, kind="ExternalInput").ap()
    NL = n_layers
    N2 = max(1, (NL + 1) // 2)
    wmod_d = dt("w_mod", [NL, D, 9 * D], F32, kind="ExternalInput").ap()
    bmod_d = dt("bmodT", [NL, 128, 72], F32, kind="ExternalInput").ap()
    g_d = dt("gT", [NL, 128, 6, 8], F32, kind="ExternalInput").ap()
    wgu_d = dt("ffn_w_gu", [NL, 2, D, 2 * DFF], F32, kind="ExternalInput").ap()
    wdn_d = dt("ffn_w_down", [NL, 2, DFF, D], F32, kind="ExternalInput").ap()
    cwin_d = dt("conv_w_in", [N2, D, 3 * D], F32, kind="ExternalInput").ap()
    cw_d = dt("convwT", [N2, 128, 3, 8], F32, kind="ExternalInput").ap()
    cwout_d = dt("conv_w_out", [N2, D, D], F32, kind="ExternalInput").ap()
    wqk_d = dt("wqk", [N2, D, 10, 2, 128], F32, kind="ExternalInput").ap()
    wv_d = dt("wv", [N2, D, 256], F32, kind="ExternalInput").ap()
    wo_d = dt("wo", [N2, D, D], F32, kind="ExternalInput").ap()
    sink_d = dt("sinkB", [N2, 128, 16], F32, kind="ExternalInput").ap()
    cos_d = dt("cosT", [128, NTOK], F32, kind="ExternalInput").ap()
    sin_d = dt("sinT", [128, NTOK], F32, kind="ExternalInput").ap()
    mask_d = dt("masks", [128, 2, 128], F32, kind="ExternalInput").ap()
    id_d = dt("ident", [128, 128], F32, kind="ExternalInput").ap()
    out_d = dt("out", [NLAT, D], F32, kind="ExternalOutput").ap()

    P = Prog()
    from contextlib import ExitStack
    es = ExitStack()
    sb = lambda name, shape, d: es.enter_context(nc.sbuf_tensor(name, shape, d))
    XZ = sb("XZ", [128, 8, NTOK], F32)
    AR = sb("AR", [128, ARENA_B // 4], F32)
    WAR = sb("WAR", [128, WAR_B // 4], F32)
    T = [sb(f"T{i}", [128, 512], F32) for i in range(3)]
    SQ = [sb(f"SQ{i}", [128, 512], BF16) for i in range(2)]
    RS = sb("RS", [128, 512], F32)
    RSTD = RS
    ONES = sb("ONES", [128, 128], BF16)
    ID2 = sb("ID2", [2, 2], F32)
    EPSC = sb("EPSC", [128, 1], F32)
    IDB = sb("IDB", [128, 128], BF16)
    CC = sb("CC", [128, 8, 2], F32)
    SCB = sb("SCB", [128, 8, 2], BF16)
    MROW = SQ[1][0:2, :].bitcast(F32)
    MODT2 = [sb(f"MODT{i}", [128, 72, 2], F32) for i in range(2)]
    BMOD2 = [sb(f"BMOD{i}", [128, 72], F32) for i in range(2)]
    GT2 = [sb(f"GT{i}", [128, 6, 8], F32) for i in range(2)]
    DER2 = [sb(f"DER{i}", [128, 6, 8, 2], F32) for i in range(2)]
    cur = {"p": 0}
    CW = sb("CW", [128, 3, 8], F32)
    ESINK = sb("ESINK", [128, 16], F32)
    PS = [es.enter_context(nc.psum_tensor(f"PS{i}", [128, 512], F32)) for i in range(8)]

    def arena(off_b, nbytes, d, pattern=None, **kw):
        v = AR[:, off_b // 4:(off_b + nbytes) // 4]
        if d == BF16:
            v = v.bitcast(BF16)
        if pattern:
            v = v.rearrange(pattern, **kw)
        return v

    def war(off_b, nbytes, d, pattern=None, **kw):
        v = WAR[:, off_b // 4:(off_b + nbytes) // 4]
        if d == BF16:
            v = v.bitcast(BF16)
        if pattern:
            v = v.rearrange(pattern, **kw)
        return v

    psk = lambda b: ("ps", b)

    P.op("dve", lambda e: e.memset(ONES[:], 1.0), w=[("ones",)])
    P.op("dve", lambda e: e.memset(EPSC[:], EPS), w=[("eps",)])
    IDENT = AR[:, 2048:2176]
    P.dma("sp", lambda e: e.dma_start(out=IDENT, in_=id_d), "c0", w=[("ident",)])
    P.dma("sp", lambda e: e.dma_start(out=ID2[:], in_=id_d[0:2, 0:2]), "c9", w=[("id2",)])
    P.dma("sp", lambda e: e.dma_start(out=CC[:], in_=cc_d), "c1", w=[("cc",)])
    P.op("dve", lambda e: e.tensor_copy(IDB[:], IDENT), r=[("ident",)], w=[("idb",)])
    P.op("act", lambda e: e.activation(SCB[:], CC[:], AF.Silu), r=[("cc",)], w=[("scb",)])

    ST = [arena(i * 4096, 4096, F32) for i in range(2)]
    for blk in range(NTOK // 128):
        s = blk % 2
        tile_i = min(blk // 4, 4)
        P.dma("sp", lambda e, s=s, blk=blk: e.dma_start(out=ST[s], in_=xz_d[blk * 128:(blk + 1) * 128, :]),
              f"st{s}", w=[("st", s)])
        for hb in range(2):
            bank = hb + 2 * (blk % 2)

            def tr(e, s=s, hb=hb, bank=bank):
                ins = None
                for cc in range(4):
                    c = hb * 4 + cc
                    ins = e.matmul(PS[bank][:, cc * 128:(cc + 1) * 128], ST[s][:, c * 128:(c + 1) * 128],
                                   IDENT, start=True, stop=True)
                return ins
            P.op("pe", tr, r=[("st", s), ("ident",)], w=[psk(bank)])
            dst = XZ[:, hb * 4:(hb + 1) * 4, blk * 128:(blk + 1) * 128]
            src = PS[bank][:].rearrange("p (c n) -> p c n", c=4)
            eng = "act" if hb == 0 else "dve"
            if eng == "act":
                P.op("act", lambda e, dst=dst, src=src: e.activation(dst, src, AF.Copy), r=[psk(bank)],
                     w=[("xz", c, tile_i, blk % 4) for c in range(hb * 4, hb * 4 + 4)])
            else:
                P.op("dve", lambda e, dst=dst, src=src: e.tensor_copy(dst, src), r=[psk(bank)],
                     w=[("xz", c, tile_i, blk % 4) for c in range(hb * 4, hb * 4 + 4)])

    def xzk(c, ti):
        return [("xz", c, ti, q) for q in range(4)]

    def norm_in(li, tiles, Aidx, Bmod, hview, hkey, extra_w=(), sbanks=(6,)):
        p_ = cur["p"]
        MODT, DER = MODT2[p_], DER2[p_]
        nbk = len(sbanks)
        RSb = [(RS, ("rs",)), (T[2], ("T", 2))]

        def sq_ops(i):
            ti = tiles[i]
            s0, n = TILES[ti]
            sb = sbanks[i % nbk]
            rsb, rkey = RSb[i % nbk]
            th = []
            for c in range(8):
                th.append(lambda c=c: P.op("act", lambda e: e.activation(SQ[c % 2][:, :n], XZ[:, c, s0:s0 + n], AF.Square),
                                           r=xzk(c, ti), w=[("sq", c % 2)]))
                th.append(lambda c=c: P.op("pe", lambda e: e.matmul(PS[sb][:, :n], ONES[:], SQ[c % 2][:, :n], start=(c == 0), stop=(c == 7)),
                                           r=[("sq", c % 2), ("ones",)], w=[psk(sb)]))
            th.append(lambda: P.op("act", lambda e: e.activation(rsb[:, :n], PS[sb][:, :n], AF.Ln, bias=EPSC[:, 0:1], scale=1.0 / D),
                                   r=[psk(sb), ("eps",)], w=[rkey]))
            th.append(lambda: P.op("act", lambda e: e.activation(rsb[:, :n], rsb[:, :n], AF.Exp, scale=-0.5), r=[rkey], w=[rkey]))
            return th

        def ap_ops(i):
            ti = tiles[i]
            s0, n = TILES[ti]
            r = 1 if ti == 4 else 0
            rsb, rkey = RSb[i % nbk]
            hv = hview(ti)
            nt = 3 if nbk == 1 else 2
            th = []
            for c in range(8):
                th.append(lambda c=c: P.op("dve", lambda e: e.tensor_tensor(T[c % nt][:, :n], XZ[:, c, s0:s0 + n], rsb[:, :n], ALU.mult),
                                           r=xzk(c, ti) + [rkey], w=[("T", c % nt)]))
                th.append(lambda c=c: P.op("act", lambda e: e.activation(
                    hv[:, c, :], T[c % nt][:, :n], AF.Identity,
                    bias=MODT[:, Bmod * 8 + c, r:r + 1], scale=DER[:, Aidx, c, r:r + 1]),
                    r=[("T", c % nt), ("modt", p_), ("der", p_)],
                    w=[(hkey, c, ti)] + (list(extra_w) if (i == 0 and c == 0) else [])))
            return th

        if nbk == 1:
            for i in range(len(tiles)):
                for f_ in sq_ops(i) + ap_ops(i):
                    f_()
            return
        for f_ in sq_ops(0):
            f_()
        for i in range(len(tiles)):
            nxt = sq_ops(i + 1) if i + 1 < len(tiles) else []
            ap = ap_ops(i)
            while ap or nxt:
                for _ in range(2):
                    if nxt:
                        nxt.pop(0)()
                for _ in range(2):
                    if ap:
                        ap.pop(0)()

    def evac_stats(pb, n, ydst, ykey, Cidx, c, r, statbank, sqi):
        p_ = cur["p"]
        DER = DER2[p_]
        P.op("act", lambda e: e.activation(ydst, PS[pb][:, :n], AF.Identity, scale=DER[:, Cidx, c, r:r + 1]),
             r=[psk(pb), ("der", p_)], w=[ykey])
        P.op("act", lambda e: e.activation(SQ[sqi][:, :n], PS[pb][:, :n], AF.Square),
             r=[psk(pb)], w=[("sq", sqi)])
        return lambda: P.op("pe", lambda e: e.matmul(PS[statbank][:, :n], ONES[:], SQ[sqi][:, :n], start=(c == 0), stop=(c == 7)),
                            r=[("sq", sqi), ("ones",)], w=[psk(statbank)])

    def finish_ops(ti, yv, ykey, statbank, use_pool=True):
        s0, n = TILES[ti]
        th = []
        th.append(lambda: P.op("act", lambda e: e.activation(RS[:, :n], PS[statbank][:, :n], AF.Ln, bias=EPSC[:, 0:1], scale=1.0 / D),
                               r=[psk(statbank), ("eps",)], w=[("rs",)]))
        th.append(lambda: P.op("act", lambda e: e.activation(RS[:, :n], RS[:, :n], AF.Exp, scale=-0.5), r=[("rs",)], w=[("rs",)]))
        for c in range(8):
            th.append(lambda c=c: P.op("dve", lambda e: e.tensor_tensor(T[c % 2][:, :n], yv[:, c, :], RS[:, :n], ALU.mult),
                                       r=[ykey(c, ti), ("rs",)], w=[("T", c % 2)]))
            th.append(lambda c=c: P.op("pool" if (use_pool and c % 4 != 3) else "dve",
                                       lambda e: e.tensor_tensor(XZ[:, c, s0:s0 + n], T[c % 2][:, :n], XZ[:, c, s0:s0 + n], ALU.add),
                                       r=[("T", c % 2)] + xzk(c, ti), w=xzk(c, ti)))
        return th

    def finish_out(ti, yv, ykey, statbank):
        for f_ in finish_ops(ti, yv, ykey, statbank):
            f_()

    def mod_setup(li):
        p_ = li % 2
        P.dma("sp", lambda e: e.dma_start(out=BMOD2[p_][:], in_=bmod_d[li]), f"c2{p_}", w=[("bmod", p_)])
        P.dma("sp", lambda e: e.dma_start(out=GT2[p_][:], in_=g_d[li]), f"c3{p_}", w=[("gt", p_)])

    NMC = 36

    def mod_slot(s):
        if s < 2:
            return war(16384 + s * 5632, 4096, BF16, "p (k n) -> p k n", k=8)
        return war((s - 2) * 8192, 4096, BF16, "p (k n) -> p k n", k=8)

    def mod_keys(s):
        if s < 2:
            return [("wd", s)]
        return [("wg", s - 2, 0), ("wg", s - 2, 1)]

    def mod_dma(li, nb, nslots=2):
        s = nb % nslots
        slot = mod_slot(s)
        wsrc = wmod_d[li].rearrange("(k p) n -> p k n", p=128)
        P.dma("pool", lambda e: e.dma_start(out=slot, in_=wsrc[:, :, nb * 256:(nb + 1) * 256]),
              f"wm{s}", w=mod_keys(s))

    def mod_pe(li, nb, nslots=2):
        p_ = li % 2
        s = nb % nslots
        slot = mod_slot(s)

        def mm(e):
            ins = None
            for kc in range(8):
                ins = e.matmul(PS[7][0:2, 0:256], SCB[:, kc, :], slot[:, kc, :], start=(kc == 0), stop=(kc == 7))
            return ins
        P.op("pe", mm, r=mod_keys(s) + [("scb",)], w=[psk(7)])
        P.op("dve", lambda e: e.tensor_copy(MROW, PS[7][0:2, 0:256]), r=[psk(7)], w=[("sq", 1)])

        def trm(e):
            ins = None
            for j in range(2):
                ins = e.matmul(PS[7][:, 384 + 2 * j:384 + 2 * j + 2], MROW[0:2, j * 128:(j + 1) * 128], ID2[0:2, 0:2],
                               start=True, stop=True)
            return ins
        P.op("pe", trm, r=[("sq", 1), ("id2",)], w=[psk(7)])
        psv = PS[7][:, 384:388].rearrange("p (m r) -> p m r", r=2)
        for r in range(2):
            P.op("dve", lambda e, r=r: e.tensor_tensor(MODT2[p_][:, nb * 2:(nb + 1) * 2, r], psv[:, :, r],
                                                       BMOD2[p_][:, nb * 2:(nb + 1) * 2], ALU.add),
                 r=[psk(7), ("bmod", p_)], w=[("modt", p_, nb, r)])

    def mod_finish(li):
        p_ = li % 2
        MODT, GT, DER = MODT2[p_], GT2[p_], DER2[p_]
        allk = [("modt", p_, nb, r) for nb in range(NMC) for r in range(2)]
        spec = [(0, 1, 0, 1.0, "A"), (1, 2, 1, 0.5, "C"), (2, 4, 2, 1.0, "A"), (3, 5, 3, 1.0, "C"),
                (4, 7, 4, 1.0, "A"), (5, 8, 5, 0.5, "C")]
        first = True
        for (di, m, gi, wgt, kind) in spec:
            for r in range(2):
                rk = allk + [("gt", p_)] if first else [("modt", p_), ("gt", p_)]
                wk = [("der", p_), ("modt", p_)] if first else [("der", p_)]
                first = False
                if kind == "A":
                    P.op("dve", lambda e, di=di, m=m, gi=gi, r=r: e.scalar_tensor_tensor(
                        DER[:, di, :, r], MODT[:, m * 8:(m + 1) * 8, r], 1.0, GT[:, gi, :], ALU.add, ALU.mult),
                        r=rk, w=wk)
                else:
                    P.op("dve", lambda e, di=di, m=m, gi=gi, r=r, wgt=wgt: e.scalar_tensor_tensor(
                        DER[:, di, :, r], MODT[:, m * 8:(m + 1) * 8, r], wgt, GT[:, gi, :], ALU.mult, ALU.mult),
                        r=rk, w=wk)

    def mod_step(q):
        if q is None:
            return
        if q["pend"] is not None:
            mod_pe(q["li"], q["pend"])
            q["pend"] = None
        if q["next"] < NMC:
            mod_dma(q["li"], q["next"])
            q["pend"] = q["next"]
            q["next"] += 1

    def mod_flush_pending(q):
        if q is not None and q["pend"] is not None:
            mod_pe(q["li"], q["pend"])
            q["pend"] = None

    def ffn(li, si, Aidx, Bmod, Cidx, skip_z=False, modq=None):
        wg_src = wgu_d[li, si].rearrange("(k p) (g n) -> p k g n", p=128, g=2)
        wd_src = wdn_d[li, si].rearrange("(f p) d -> p f d", p=128)
        wgs = [war(i * 8192, 8192, BF16, "p (k g n) -> p k g n", k=8, g=2) for i in range(2)]
        wds = [war(16384 + i * 5632, 5632, BF16, "p (f n) -> p f n", f=NF) for i in range(2)]
        def group(gi, tiles, carry):
            loc = {}
            o = 0
            for t in tiles:
                loc[t] = o
                o += TILES[t][1]
            if gi == 0:
                ntok = 1280
                A_ = arena(0, NF * ntok * 2, BF16, "p (f n) -> p f n", f=NF)
                Y_ = arena(NF * ntok * 2, 8 * ntok * 4, F32, "p (c n) -> p c n", c=8)
                H_ = arena(NF * ntok * 2, 8 * ntok * 2, BF16, "p (c n) -> p c n", c=8)
            else:
                ntok = 1024
                H_ = arena(0, 8 * ntok * 2, BF16, "p (c n) -> p c n", c=8)
                A_ = arena(16384, NF * ntok * 2, BF16, "p (f n) -> p f n", f=NF)
                Y_ = arena(61440, 8 * ntok * 4, F32, "p (c n) -> p c n", c=8)
            hview = lambda ti: H_[:, :, loc[ti]:loc[ti] + TILES[ti][1]]
            yview = lambda ti: Y_[:, :, loc[ti]:loc[ti] + TILES[ti][1]]
            prev_keys, deferred = carry
            norm_in(li, tiles, Aidx, Bmod, hview, "h", extra_w=prev_keys, sbanks=((7, 6) if gi == 0 else (7,)))
            step = 0
            for fg in range(NF // 2):
                s = fg % 2
                for gu in range(2):
                    P.dma("pool", lambda e, s=s, fg=fg, gu=gu: e.dma_start(
                        out=wgs[s][:, :, gu, :], in_=wg_src[:, :, gu, fg * 256:(fg + 1) * 256]),
                        f"wg{s}{gu}", w=[("wg", s, gu)])
                if fg < 9:
                    mod_step(modq)
                elif fg == 9:
                    mod_flush_pending(modq)
                for j in range(2):
                    f = fg * 2 + j
                    if f == 18:
                        while deferred:
                            deferred.pop(0)()
                    for t in tiles:
                        n = TILES[t][1]
                        pb = (0, 4)[step % 2]
                        step += 1

                        def mmg(e, s=s, j=j, t=t, n=n, pb=pb, gu=0):
                            ins = None
                            for kc in range(8):
                                ins = e.matmul(PS[pb + gu][:, :n], wgs[s][:, kc, gu, j * 128:(j + 1) * 128],
                                               H_[:, kc, loc[t]:loc[t] + n], start=(kc == 0), stop=(kc == 7))
                            return ins
                        hk = [("h", c, t) for c in range(8)]
                        P.op("pe", mmg, r=[("wg", s, 0)] + hk, w=[psk(pb)])
                        P.op("pe", lambda e, f=mmg: f(e, gu=1), r=[("wg", s, 1)] + hk, w=[psk(pb + 1)])
                        P.op("act", lambda e, n=n, pb=pb: e.activation(T[2][:, :n], PS[pb][:, :n], AF.Silu),
                             r=[psk(pb)], w=[("T", 2)])
                        P.op("dve", lambda e, f=f, t=t, n=n, pb=pb: e.tensor_tensor(
                            A_[:, f, loc[t]:loc[t] + n], T[2][:, :n], PS[pb + 1][:, :n], ALU.mult),
                            r=[("T", 2), psk(pb + 1)], w=[("a", f, t)])
                        for _ in range(2):
                            if deferred:
                                deferred.pop(0)()
            while deferred:
                deferred.pop(0)()
            mod_flush_pending(modq)
            SB3 = [2, 3, 6]
            pend = None
            step = 0
            for dc in range(8):
                s = dc % 2
                P.dma("pool", lambda e, s=s, dc=dc: e.dma_start(out=wds[s], in_=wd_src[:, :, dc * 128:(dc + 1) * 128]),
                      f"wd{s}", w=[("wd", s)])
                for k, t in enumerate(tiles):
                    n = TILES[t][1]
                    pb = 4 + (step % 2)

                    def mmd(e, s=s, t=t, n=n, pb=pb):
                        ins = None
                        for fc in range(NF):
                            ins = e.matmul(PS[pb][:, :n], wds[s][:, fc, :], A_[:, fc, loc[t]:loc[t] + n],
                                           start=(fc == 0), stop=(fc == NF - 1))
                        return ins
                    P.op("pe", mmd, r=[("wd", s)] + [("a", f, t) for f in range(NF)], w=[psk(pb)])
                    nxt = evac_stats(pb, n, Y_[:, dc, loc[t]:loc[t] + n], ("y", dc, t), Cidx, dc, 1 if t == 4 else 0,
                                     SB3[k], step % 2)
                    step += 1
                    if pend is not None:
                        pend()
                    pend = nxt
            pend()
            fin = []
            for k, t in enumerate(tiles):
                fin += finish_ops(t, yview(t), lambda c, ti: ("y", c, ti), SB3[k], use_pool=(gi == len(GROUPS) - 1))
            akeys = [("a", f, t) for f in range(NF) for t in tiles]
            return (akeys, fin)

        carry = ([], [])
        for gi, grp in enumerate(GROUPS):
            carry = group(gi, [t for t in grp if not (skip_z and t == 4)], carry)
        for f_ in carry[1]:
            f_()
        P.fence()

    def conv_mixer(li, j, update_z=True):
        HX = arena(0, 36864, BF16, "p (c n) -> p c n", c=8)
        VB = arena(36864, 36864, BF16, "p (c n) -> p c n", c=8)
        CU = arena(73728, 9232, F32)
        BB = arena(82960, 9232, F32)
        WO = arena(0, 16384, BF16, "p (k n) -> p k n", k=8)
        YT = arena(16384, 16384, F32, "p (c n) -> p c n", c=8)
        wis = [war(i * 6144, 6144, BF16, "p (k g n) -> p k g n", k=8, g=3) for i in range(2)]
        tiles = [0, 1, 2, 3, 4]
        P.dma("sp", lambda e: e.dma_start(out=CW[:], in_=cw_d[j]), "c4", w=[("cw",)])
        norm_in(li, tiles, 2, 3, lambda ti: HX[:, :, TILES[ti][0]:TILES[ti][0] + TILES[ti][1]], "h", sbanks=(6, 7))
        P.op("dve", lambda e: e.memset(CU[:], 0.0), w=[("cu", t) for t in tiles] + [("cupad",)])
        yi = lambda t: TILES[t][0] if t < 4 else 2050
        win_src = cwin_d[j].rearrange("(k p) (g n) -> p k g n", p=128, g=3)
        for fc in range(8):
            s = fc % 2
            for g3 in range(3):
                P.dma("pool", lambda e, s=s, fc=fc, g3=g3: e.dma_start(
                    out=wis[s][:, :, g3, :], in_=win_src[:, :, g3, fc * 128:(fc + 1) * 128]),
                    f"wi{s}{g3}", w=[("wi", s, g3)])
            for t in tiles:
                s0, n = TILES[t]
                pb = 3 * ((fc * 5 + t) % 2)
                hk = [("h", c, t) for c in range(8)]
                for g3 in range(3):
                    def mmc(e, s=s, g3=g3, s0=s0, n=n, pb=pb):
                        ins = None
                        for kc in range(8):
                            ins = e.matmul(PS[pb + g3][:, :n], wis[s][:, kc, g3, :], HX[:, kc, s0:s0 + n],
                                           start=(kc == 0), stop=(kc == 7))
                        return ins
                    P.op("pe", mmc, r=[("wi", s, g3)] + hk, w=[psk(pb + g3)])
                y0 = yi(t)
                P.op("act", lambda e, y0=y0, n=n, pb=pb: e.activation(BB[:, y0:y0 + n], PS[pb][:, :n], AF.Copy),
                     r=[psk(pb)], w=[("bb", t)])
                P.op("act", lambda e, y0=y0, n=n, pb=pb: e.activation(CU[:, y0 + 1:y0 + 1 + n], PS[pb + 1][:, :n], AF.Copy),
                     r=[psk(pb + 1)], w=[("cu", t)])
                P.op("dve", lambda e, y0=y0, n=n, pb=pb: e.tensor_tensor(
                    CU[:, y0 + 1:y0 + 1 + n], CU[:, y0 + 1:y0 + 1 + n], PS[pb + 2][:, :n], ALU.mult),
                    r=[("cu", t), psk(pb + 2)], w=[("cu", t)])
            for t in tiles:
                s0, n = TILES[t]
                y0 = yi(t)
                nb = [("cu", t), ("cupad",), ("cw",)]
                if 0 < t < 4:
                    nb.append(("cu", t - 1))
                if t < 3:
                    nb.append(("cu", t + 1))
                P.op("act", lambda e, fc=fc, y0=y0, n=n: e.activation(T[0][:, :n], CU[:, y0 + 1:y0 + 1 + n], AF.Identity,
                                                                   scale=CW[:, 1, fc:fc + 1]),
                     r=nb, w=[("T", 0)])
                P.op("act", lambda e, fc=fc, y0=y0, n=n: e.activation(T[1][:, :n], CU[:, y0:y0 + n], AF.Identity,
                                                                   scale=CW[:, 0, fc:fc + 1]),
                     r=nb, w=[("T", 1)])
                P.op("act", lambda e, fc=fc, y0=y0, n=n: e.activation(T[2][:, :n], CU[:, y0 + 2:y0 + 2 + n], AF.Identity,
                                                                   scale=CW[:, 2, fc:fc + 1]),
                     r=nb, w=[("T", 2)])
                P.op("dve", lambda e, n=n: e.tensor_tensor(T[0][:, :n], T[0][:, :n], T[1][:, :n], ALU.add),
                     r=[("T", 0), ("T", 1)], w=[("T", 0)])
                P.op("dve", lambda e, n=n: e.tensor_tensor(T[0][:, :n], T[0][:, :n], T[2][:, :n], ALU.add),
                     r=[("T", 0), ("T", 2)], w=[("T", 0)])
                P.op("dve", lambda e, fc=fc, s0=s0, y0=y0, n=n: e.tensor_tensor(VB[:, fc, s0:s0 + n], T[0][:, :n], BB[:, y0:y0 + n], ALU.mult),
                     r=[("T", 0), ("bb", t)], w=[("v", fc, t)])
        P.fence()
        P.dma("pool", lambda e: e.dma_start(out=WO, in_=cwout_d[j].rearrange("(k p) d -> p k d", p=128)), "wo", w=[("wo",)])
        YTs = [YT, arena(73728, 16384, F32, "p (c n) -> p c n", c=8)]
        for t in (tiles if update_z else tiles[:4]):
            s0, n = TILES[t]
            YT = YTs[t % 2]
            ytn = "yt%d" % (t % 2)
            pend = None
            for dc in range(8):
                pb = dc % 2

                def mmo(e, dc=dc, s0=s0, n=n, pb=pb):
                    ins = None
                    for fc in range(8):
                        ins = e.matmul(PS[pb][:, :n], WO[:, fc, dc * 128:(dc + 1) * 128], VB[:, fc, s0:s0 + n],
                                       start=(fc == 0), stop=(fc == 7))
                    return ins
                P.op("pe", mmo, r=[("wo",)] + [("v", fc, t) for fc in range(8)], w=[psk(pb)])
                nxt = evac_stats(pb, n, YT[:, dc, :n], (ytn, dc), 3, dc, 1 if t == 4 else 0, 6, dc % 2)
                if pend is not None:
                    pend()
                pend = nxt
            pend()
            finish_out(t, YT[:, :, :n], lambda c, ti, ytn=ytn: (ytn, c), 6)
        P.fence()

    def attn_mixer(li, j, ctx_queries):
        HX = arena(0, 36864, BF16, "p (c n) -> p c n", c=8)
        COS = arena(36864, 9216, F32)
        SIN = arena(46080, 9216, F32)
        KT = arena(55296, 9216, BF16, "p (c n) -> p c n", c=2)
        VV = arena(64512, 9216, BF16, "p (b n) -> p b n", b=18)
        QR = arena(73728, 8192, BF16, "p (c n) -> p c n", c=8)
        OC = arena(81920, 8192, BF16, "p (c n) -> p c n", c=8)
        PT = [arena(90112 + i * 1024, 1024, BF16) for i in range(3)]
        wsl = [war(i * 4096, 4096, BF16, "p (k g n) -> p k g n", k=8, g=2) for i in range(2)]
        YT = war(8192, 16384, F32, "p (c n) -> p c n", c=8)
        MK = war(24576, 512, BF16, "p (m n) -> p m n", m=2)
        tiles = [0, 1, 2, 3, 4]
        P.dma("sp", lambda e: e.dma_start(out=COS, in_=cos_d), "c5", w=[("cos",)])
        P.dma("sp", lambda e: e.dma_start(out=SIN, in_=sin_d), "c6", w=[("sin",)])
        P.dma("pool", lambda e: e.dma_start(out=MK, in_=mask_d), "c7", w=[("mk",)])
        P.dma("sp", lambda e: e.dma_start(out=ESINK[:], in_=sink_d[j]), "c8", w=[("esink",)])
        P.op("act", lambda e: e.activation(ESINK[:], ESINK[:], AF.Exp), r=[("esink",)], w=[("esink",)])
        norm_in(li, tiles, 2, 3, lambda ti: HX[:, :, TILES[ti][0]:TILES[ti][0] + TILES[ti][1]], "h", sbanks=(6, 7))
        wsrc = wqk_d[j].rearrange("(k p) c g n -> p k c g n", p=128)

        def load_qk(c, s):
            for g2 in range(2):
                P.dma("pool", lambda e, c=c, s=s, g2=g2: e.dma_start(out=wsl[s][:, :, g2, :], in_=wsrc[:, :, c, g2, :]),
                      f"wq{s}{g2}", w=[("wq", s, g2)])

        def proj_rope(c_w, s, t, dst, dkey, pb=4):
            s0, n = TILES[t]
            hk = [("h", c, t) for c in range(8)]
            for g2 in range(2):
                def mmq(e, g2=g2, s=s, s0=s0, n=n):
                    ins = None
                    for kc in range(8):
                        ins = e.matmul(PS[pb + g2][:, :n], wsl[s][:, kc, g2, :], HX[:, kc, s0:s0 + n],
                                       start=(kc == 0), stop=(kc == 7))
                    return ins
                P.op("pe", mmq, r=[("wq", s, g2)] + hk, w=[psk(pb + g2)])
            P.op("dve", lambda e, s0=s0, n=n: e.tensor_tensor(T[0][:, :n], PS[pb][:, :n], COS[:, s0:s0 + n], ALU.mult),
                 r=[psk(pb), ("cos",)], w=[("T", 0)])
            P.op("dve", lambda e, s0=s0, n=n: e.tensor_tensor(T[1][:, :n], PS[pb + 1][:, :n], SIN[:, s0:s0 + n], ALU.mult),
                 r=[psk(pb + 1), ("sin",)], w=[("T", 1)])
            P.op("dve", lambda e, n=n, dst=dst: e.tensor_tensor(dst, T[0][:, :n], T[1][:, :n], ALU.add),
                 r=[("T", 0), ("T", 1)], w=[dkey])

        for c2 in range(2):
            load_qk(8 + c2, c2 % 2)
            for t in tiles:
                s0, n = TILES[t]
                proj_rope(8 + c2, c2 % 2, t, KT[:, c2, s0:s0 + n], ("kt", c2, t), pb=(4, 0)[t % 2])
        WV = war(0, 4096, BF16, "p (k n) -> p k n", k=8)
        P.dma("pool", lambda e: e.dma_start(out=WV, in_=wv_d[j].rearrange("(k p) n -> p k n", p=128)), "wvv",
              w=[("wq", 0, 0), ("wq", 0, 1)])
        for blk in range(18):
            t = min(blk // 4, 4)
            pb = blk % 2

            def mmv(e, blk=blk, pb=pb):
                ins = None
                for kc in range(8):
                    ins = e.matmul(PS[pb][:, 0:256], HX[:, kc, blk * 128:(blk + 1) * 128], WV[:, kc, :],
                                   start=(kc == 0), stop=(kc == 7))
                return ins
            P.op("pe", mmv, r=[("wq", 0, 0), ("wq", 0, 1)] + [("h", c, t) for c in range(8)], w=[psk(pb)])
            P.op("act", lambda e, blk=blk, pb=pb: e.activation(VV[:, blk, :], PS[pb][:, 0:256], AF.Copy),
                 r=[psk(pb)], w=[("vv", blk)])
        wo_src = wo_d[j].rearrange("(k p) d -> p k d", p=128)
        qtiles = tiles if ctx_queries else tiles[:4]
        pti = 0
        for t in qtiles:
            s0, n = TILES[t]
            for c in range(8):
                load_qk(c, c % 2)
                proj_rope(c, c % 2, t, QR[:, c, :n], ("qr", c), pb=(4, 0)[c % 2])
            n0 = s0 // 128
            steps = []
            for c in range(8):
                for half in range(2):
                    h = (HE if half == 0 else HO)[c]
                    g = h // 4
                    kbs = [(16, 0, n), (17, 0, n)]
                    if t < 4:
                        for m in range(max(0, n0 - 1), min(15, n0 + 4) + 1):
                            lo = max(m - 1, n0)
                            hi = min(m + 1, n0 + 3)
                            kbs.append((m, (lo - n0) * 128, (hi - n0 + 1) * 128))
                    for ki, (kb, qa, qb) in enumerate(kbs):
                        steps.append(dict(c=c, half=half, h=h, kc2=g // 2, kb=kb, qa=qa, qb=qb,
                                          first=(ki == 0), last=(ki == len(kbs) - 1)))
            SBK = [4, 5, 7]

            def emit_S(i):
                st = steps[i]
                sbk = SBK[i % 3]
                Pp = slice(st["half"] * 64, st["half"] * 64 + 64)
                kb, qa, qb, c, kc2 = st["kb"], st["qa"], st["qb"], st["c"], st["kc2"]
                mk = []
                if kb < 16:
                    if n0 <= kb - 1 <= n0 + 3:
                        mk.append(((kb - 1 - n0) * 128, 0))
                    if n0 <= kb + 1 <= n0 + 3:
                        mk.append(((kb + 1 - n0) * 128, 1))

                def f(e, kb=kb, qa=qa, qb=qb, sbk=sbk, Pp=Pp, kc2=kc2, c=c, mk=mk):
                    ins = e.matmul(PS[sbk][:, qa:qb], KT[Pp, kc2, kb * 128:(kb + 1) * 128], QR[Pp, c, qa:qb],
                                   start=True, stop=(len(mk) == 0))
                    for mi, (a0, which) in enumerate(mk):
                        ins = e.matmul(PS[sbk][:, a0:a0 + 128], IDB[:], MK[:, which, :], start=False,
                                       stop=(mi == len(mk) - 1))
                    return ins
                P.op("pe", f, r=[("kt", kc2, min(kb // 4, 4)), ("qr", c), ("mk",), ("idb",)], w=[psk(sbk)])

            def emit_exp(i):
                st = steps[i]
                sbk = SBK[i % 3]
                pt = PT[i % 3]
                P.op("act", lambda e, qa=st["qa"], qb=st["qb"], sbk=sbk, pt=pt: e.activation(
                    pt[:, qa:qb], PS[sbk][:, qa:qb], AF.Exp, scale=0.125),
                    r=[psk(sbk)], w=[("pt", i % 3)])

            def emit_PV(i):
                st = steps[i]
                pt = PT[i % 3]
                ptk = ("pt", i % 3)
                half, c, h, kb, qa, qb, kc2 = st["half"], st["c"], st["h"], st["kb"], st["qa"], st["qb"], st["kc2"]
                first, last = st["first"], st["last"]
                ob = 2 * half
                Pp = slice(half * 64, half * 64 + 64)
                P.op("pe", lambda e, kb=kb, qa=qa, qb=qb, ob=ob, pt=pt, kc2=kc2, first=first, last=last: e.matmul(
                    PS[ob][:, qa:qb], VV[:, kb, kc2 * 128:(kc2 + 1) * 128], pt[:, qa:qb],
                    start=first, stop=last),
                    r=[ptk, ("vv", kb)], w=[psk(ob)])
                P.op("pe", lambda e, qa=qa, qb=qb, ob=ob, pt=pt, first=first, last=last: e.matmul(
                    PS[ob + 1][:, qa:qb], ONES[:], pt[:, qa:qb], start=first, stop=last),
                    r=[ptk, ("ones",)], w=[psk(ob + 1)])
                if last:
                    normq.append((i + 3, lambda Pp=Pp, ob=ob, h=h, half=half, c=c: emit_norm(Pp, ob, h, half, c)))

            def emit_norm(Pp, ob, h, half, c):
                if True:
                    tk = ("T", half)
                    P.op("act", lambda e, Pp=Pp, n=n, ob=ob, h=h, half=half: e.activation(
                        T[half][Pp, :n], PS[ob + 1][Pp, :n], AF.Ln, bias=ESINK[Pp, h:h + 1], scale=1.0),
                        r=[psk(ob + 1), ("esink",)], w=[tk])
                    P.op("act", lambda e, Pp=Pp, n=n, half=half: e.activation(T[half][Pp, :n], T[half][Pp, :n], AF.Exp, scale=-1.0),
                         r=[tk], w=[tk])
                    P.op("dve", lambda e, Pp=Pp, n=n, ob=ob, c=c, half=half: e.tensor_tensor(
                        OC[Pp, c, :n], PS[ob][Pp, :n], T[half][Pp, :n], ALU.mult),
                        r=[psk(ob), tk], w=[("oc", c, half)])

            LOOK = 2
            ns = len(steps)
            normq = []
            for i in range(min(LOOK, ns)):
                emit_S(i)
            for i in range(ns):
                emit_exp(i)
                while normq and normq[0][0] <= i:
                    normq.pop(0)[1]()
                if i + LOOK < ns:
                    emit_S(i + LOOK)
                emit_PV(i)
            while normq:
                normq.pop(0)[1]()
            pend = None
            for dc in range(8):
                s = (dc % 4) // 2
                g2 = dc % 2
                P.dma("pool", lambda e, s=s, g2=g2, dc=dc: e.dma_start(out=wsl[s][:, :, g2, :], in_=wo_src[:, :, dc * 128:(dc + 1) * 128]),
                      f"wq{s}{g2}", w=[("wq", s, g2)])
                pbo = 4 + dc % 2

                def mmo(e, s=s, g2=g2, n=n, pbo=pbo):
                    ins = None
                    for kc in range(8):
                        ins = e.matmul(PS[pbo][:, :n], wsl[s][:, kc, g2, :], OC[:, kc, :n], start=(kc == 0), stop=(kc == 7))
                    return ins
                P.op("pe", mmo, r=[("wq", s, g2)] + [("oc", c, hh) for c in range(8) for hh in range(2)], w=[psk(pbo)])
                nxt = evac_stats(pbo, n, YT[:, dc, :n], ("yt", dc), 3, dc, 1 if t == 4 else 0, 6, dc % 2)
                if pend is not None:
                    pend()
                pend = nxt
            pend()
            finish_out(t, YT[:, :, :n], lambda c, ti: ("yt", c), 6)
        P.fence()

    mod_setup(0)
    for nb in range(min(3, NMC)):
        mod_dma(0, nb, 4)
    for nb in range(NMC):
        if nb + 3 < NMC:
            mod_dma(0, nb + 3, 4)
        mod_pe(0, nb, 4)
    mod_finish(0)
    P.fence()
    for li in range(n_layers):
        last = li == DEPTH - 1
        use_attn = ((li % 2) == 1) or force_attn
        j = li // 2
        cur["p"] = li % 2
        q = None
        if li + 1 < n_layers:
            q = {"li": li + 1, "next": 0, "pend": None}
            mod_setup(li + 1)
        if stop_after == (li, "mod"):
            break
        ffn(li, 0, 0, 0, 1, modq=q)
        if stop_after == (li, "ffn0"):
            break
        if use_attn:
            attn_mixer(li, j, ctx_queries=not last)
        else:
            conv_mixer(li, j, update_z=not last)
        if stop_after == (li, "mix"):
            break
        ffn(li, 1, 4, 6, 5, skip_z=last, modq=q)
        if q is not None:
            while q["next"] < NMC or q["pend"] is not None:
                mod_step(q)
            mod_finish(li + 1)

    P.fence()
    P.dma("sp", lambda e: e.dma_start(out=IDENT, in_=id_d), "c0", w=[("ident",)])
    OT = [arena(i * 4096, 4096, F32) for i in range(2)]
    for blk in range(NLAT // 128):
        s = blk % 2
        ti = blk // 4
        for hb in range(2):
            bank = hb + 2 * s

            def tro(e, hb=hb, bank=bank, blk=blk):
                ins = None
                for cc in range(4):
                    c = hb * 4 + cc
                    ins = e.matmul(PS[bank][:, cc * 128:(cc + 1) * 128], XZ[:, c, blk * 128:(blk + 1) * 128],
                                   IDENT, start=True, stop=True)
                return ins
            P.op("pe", tro, r=[("ident",)] + [k for c in range(hb * 4, hb * 4 + 4) for k in xzk(c, ti)], w=[psk(bank)])
            if hb == 0:
                P.op("act", lambda e, s=s, bank=bank: e.activation(OT[s][:, 0:512], PS[bank][:], AF.Copy),
                     r=[psk(bank)], w=[("ot", s, 0)])
            else:
                P.op("dve", lambda e, s=s, bank=bank: e.tensor_copy(OT[s][:, 512:1024], PS[bank][:]),
                     r=[psk(bank)], w=[("ot", s, 1)])
        P.dma("sp", lambda e, s=s, blk=blk: e.dma_start(out=out_d[blk * 128:(blk + 1) * 128, :], in_=OT[s]),
              f"o{s}", r=[("ot", s, 0), ("ot", s, 1)], w=[("out", blk)])
    P.op("sp", None, r=[("out", b) for b in range(NLAT // 128)])

    P.finalize()
    sems = {}
    for e in Prog.ENGS:
        sems[("e", e)] = es.enter_context(nc.semaphore(f"sem_{e}"))
    for d in P.dsems:
        sems[("d", d)] = es.enter_context(nc.semaphore(f"dsem_{d}"))
    with nc.Block() as block:
        P.emit(nc, block, sems)
    es.close()
    return nc, P


def _rope_tables():
    p = np.arange(128)
    d = p % 64
    a = d // 32
    half = (d % 32) // 16
    i = d % 16
    inv_freq = (10000.0 ** (-np.arange(16, dtype=np.float32) / 16)).astype(np.float32)
    tok = np.arange(NLAT)
    row = (tok // 64).astype(np.float32)
    col = (tok % 64).astype(np.float32)
    pos = np.where(a[:, None] == 0, row[None, :], col[None, :]).astype(np.float32)
    ang = (pos * inv_freq[i][:, None]).astype(np.float32)
    cos = np.cos(ang).astype(np.float32)
    sin = np.sin(ang).astype(np.float32)
    sgn = np.where(half == 0, -1.0, 1.0).astype(np.float32)[:, None]
    cosT = np.ones((128, NTOK), np.float32)
    sinT = np.zeros((128, NTOK), np.float32)
    cosT[:, :NLAT] = cos
    sinT[:, :NLAT] = sin * sgn
    return cosT, sinT


def _swap_cols(w):
    sh = w.shape
    w4 = w.reshape(sh[:-1] + (sh[-1] // 32, 2, 16))
    return np.ascontiguousarray(w4[..., ::-1, :]).reshape(sh)


def prepare_inputs(x, c, ctx, c_ctx, w_mod, b_mod, norm_g, ffn_w_gu, ffn_w_down,
                   conv_w_in, conv_w, conv_w_out, attn_w_qkv, attn_w_o, attn_sink, cores=None, n_layers=DEPTH):
    f = lambda a: np.ascontiguousarray(np.asarray(a, dtype=np.float32))
    x, c, ctx, c_ctx = f(x), f(c), f(ctx), f(c_ctx)
    w_mod, b_mod, norm_g = f(w_mod), f(b_mod), f(norm_g)
    attn_w_qkv, attn_w_o, attn_sink = f(attn_w_qkv), f(attn_w_o), f(attn_sink)
    conv_w = f(conv_w)
    B = x.shape[0]
    cosT, sinT = _rope_tables()
    kk = np.arange(128)
    masks = np.zeros((128, 2, 128), np.float32)
    masks[:, 0, :] = np.where(kk[:, None] <= kk[None, :], 0.0, -30000.0)
    masks[:, 1, :] = np.where(kk[:, None] >= kk[None, :], 0.0, -30000.0)
    ident = np.eye(128, dtype=np.float32)
    bmodT = f(b_mod.reshape(-1, 72, 128).transpose(0, 2, 1))
    gT = f(norm_g.reshape(-1, 6, 8, 128).transpose(0, 3, 1, 2))
    convwT = f(conv_w.reshape(-1, 3, 8, 128).transpose(0, 3, 1, 2))
    nA = attn_w_qkv.shape[0]
    wq = attn_w_qkv[:, :, :1024].reshape(nA, D, 16, 64)
    wk = attn_w_qkv[:, :, 1024:1280].reshape(nA, D, 4, 64)
    chunks = []
    for cch in range(8):
        chunks.append(np.concatenate([wq[:, :, HE[cch]], wq[:, :, HO[cch]]], axis=-1))
    for c2 in range(2):
        chunks.append(np.concatenate([wk[:, :, 2 * c2], wk[:, :, 2 * c2 + 1]], axis=-1))
    wqk = np.stack(chunks, axis=2)
    wqk = np.stack([wqk, _swap_cols(wqk)], axis=3)
    wv = f(attn_w_qkv[:, :, 1280:1536])
    rows = []
    for cch in range(8):
        for h in (HE[cch], HO[cch]):
            rows.append(np.arange(h * 64, h * 64 + 64))
    rows = np.concatenate(rows)
    wo = f(attn_w_o[:, rows, :])
    sinkB = f(np.broadcast_to(attn_sink[:, None, :], (nA, 128, 16)))
    shared = {
        "w_mod": w_mod, "bmodT": bmodT, "gT": gT, "ffn_w_gu": f(ffn_w_gu), "ffn_w_down": f(ffn_w_down),
        "conv_w_in": f(conv_w_in), "convwT": convwT, "conv_w_out": f(conv_w_out),
        "wqk": f(wqk), "wv": wv, "wo": wo, "sinkB": sinkB, "cosT": cosT, "sinT": sinT,
        "masks": masks, "ident": ident,
    }
    if n_layers != DEPTH:
        N2 = max(1, (n_layers + 1) // 2)
        for k in ("w_mod", "bmodT", "gT", "ffn_w_gu", "ffn_w_down"):
            shared[k] = np.ascontiguousarray(shared[k][:n_layers])
        for k in ("conv_w_in", "convwT", "conv_w_out", "wqk", "wv", "wo", "sinkB"):
            shared[k] = np.ascontiguousarray(shared[k][:N2])
    in_maps = []
    for b in (range(B) if cores is None else cores):
        m = dict(shared)
        m["xz"] = f(np.concatenate([x[b], ctx[b]], axis=0))
        cc2 = np.stack([c[b], c_ctx], axis=0)
        m["ccT"] = f(cc2.reshape(2, 8, 128).transpose(2, 1, 0))
        in_maps.append(m)
    return in_maps


_CACHE = {}


def kernel(x, c, ctx, c_ctx, w_mod, b_mod, norm_g, ffn_w_gu, ffn_w_down,
           conv_w_in, conv_w, conv_w_out, attn_w_qkv, attn_w_o, attn_sink):
    in_maps = prepare_inputs(x, c, ctx, c_ctx, w_mod, b_mod, norm_g, ffn_w_gu, ffn_w_down,
                             conv_w_in, conv_w, conv_w_out, attn_w_qkv, attn_w_o, attn_sink)
    if "nc" not in _CACHE:
        _CACHE["nc"] = build_program()[0]
    nc = _CACHE["nc"]
    res = run_bass_kernel_spmd(nc, in_maps, core_ids=list(range(len(in_maps))))
    out = np.stack([np.asarray(r["out"], dtype=np.float32) for r in res.results], axis=0)
    return out
```
